# Optimizing a Trainium2 kernel written in Bass

```python
import jax, jax.numpy as jnp
from jax import lax
import numpy as np

D_MODEL = 4096
BATCH = 2
SEQ = 8192
DEPTH = 1

CHUNK = 64

RMS_EPS = 1e-6
LRU_WIDTH = D_MODEL
LRU_HEADS = 16
LRU_HEAD_DIM = LRU_WIDTH // LRU_HEADS
CONV_WIDTH = 4
LRU_C = 8.0
POOL_WINDOWS = (2, 4, 8, 16)
POOL_GROUPS = len(POOL_WINDOWS)
POOL_WIDTH = D_MODEL
POOL_GROUP_DIM = POOL_WIDTH // POOL_GROUPS
N_BRANCHES = 2
IN_WIDTH = 2 * LRU_WIDTH + POOL_WIDTH + N_BRANCHES * D_MODEL
D_FF = -(-8 * D_MODEL // (3 * 256)) * 256
N_MOD = 6

kernel_name = "hybrid_rglru_pool_swiglu_block"


def _rmsnorm(x, g):
    x32 = x.astype(jnp.float32)
    y = x32 * lax.rsqrt(jnp.mean(x32 * x32, axis=-1, keepdims=True) + RMS_EPS)
    return y.astype(x.dtype) * g


def _modulate(u, shift, scale):
    return u * (1 + scale[:, None, :]) + shift[:, None, :]


def _causal_depthwise_conv(x, w, b):
    S = x.shape[1]
    xp = jnp.pad(x, ((0, 0), (CONV_WIDTH - 1, 0), (0, 0)))
    y = b
    for k in range(CONV_WIDTH):
        y = y + w[k] * xp[:, k:k + S]
    return y


def _block_diag_linear(x, w, b):
    B, S, _ = x.shape
    xh = x.reshape(B, S, LRU_HEADS, LRU_HEAD_DIM)
    return jnp.einsum('bshi,hij->bshj', xh, w).reshape(B, S, LRU_WIDTH) + b


def _rg_lru(x, w_a, b_a, w_x, b_x, lam):
    B, S, _ = x.shape
    r = jax.nn.sigmoid(_block_diag_linear(x, w_a, b_a).astype(jnp.float32))
    i = jax.nn.sigmoid(_block_diag_linear(x, w_x, b_x).astype(jnp.float32))
    log_a = -LRU_C * r * jax.nn.softplus(-lam.astype(jnp.float32))
    a = jnp.exp(log_a)
    u = jnp.sqrt(-jnp.expm1(2.0 * log_a)) * (i * x.astype(jnp.float32))

    def step(h, inp):
        a_t, u_t = inp
        h = a_t * h + u_t
        return h, h

    h0 = jnp.zeros((B, LRU_WIDTH), jnp.float32)
    _, hs = lax.scan(step, h0, (jnp.swapaxes(a, 0, 1), jnp.swapaxes(u, 0, 1)))
    return jnp.swapaxes(hs, 0, 1).astype(x.dtype)


def _multiscale_pool(p, pool_w, pool_scale):
    B, S, _ = p.shape
    p32 = p.astype(jnp.float32).reshape(B, S, POOL_GROUPS, POOL_GROUP_DIM)
    cs = jnp.cumsum(p32, axis=1)
    t = jnp.arange(S)
    outs = []
    for g, w in enumerate(POOL_WINDOWS):
        csg = cs[:, :, g]
        prev = jnp.pad(csg, ((0, 0), (w, 0), (0, 0)))[:, :S]
        cnt = jnp.minimum(t + 1, w).astype(jnp.float32)[None, :, None]
        outs.append((csg - prev) / cnt)
    pooled = (jnp.stack(outs, axis=2) - p32).astype(p.dtype)
    mixed = jnp.einsum('bsgi,gij->bsgj', pooled, pool_w).reshape(B, S, POOL_WIDTH)
    return mixed * pool_scale


def _normal(k, shape, fan_in):
    return jax.random.normal(k, shape, jnp.float32) * (fan_in ** -0.5)


def setup_inputs(seed: int = 0) -> dict:
    key = jax.random.key(seed)
    ks = jax.random.split(key, 24)
    L = DEPTH
    u = jax.random.uniform(ks[11], (L, LRU_WIDTH), jnp.float32, 0.9, 0.999)
    a0 = u ** (1.0 / LRU_C)
    lru_lambda = jnp.log(a0) - jnp.log1p(-a0)
    return {
        "x": jax.random.normal(ks[0], (BATCH, SEQ, D_MODEL), jnp.float32),
        "c": jax.random.normal(ks[1], (BATCH, D_MODEL), jnp.float32),
        "w_ada": _normal(ks[2], (L, D_MODEL, N_MOD * D_MODEL), D_MODEL),
        "b_ada": 0.02 * jax.random.normal(ks[3], (L, N_MOD * D_MODEL), jnp.float32),
        "g_mix_pre": 1.0 + 0.05 * jax.random.normal(ks[4], (L, D_MODEL), jnp.float32),
        "g_mix_post": 1.0 + 0.05 * jax.random.normal(ks[5], (L, D_MODEL), jnp.float32),
        "w_in": _normal(ks[6], (L, D_MODEL, IN_WIDTH), D_MODEL),
        "conv_w": _normal(ks[7], (L, CONV_WIDTH, LRU_WIDTH), CONV_WIDTH),
        "conv_b": 0.02 * jax.random.normal(ks[8], (L, LRU_WIDTH), jnp.float32),
        "w_rg_a": _normal(ks[9], (L, LRU_HEADS, LRU_HEAD_DIM, LRU_HEAD_DIM), LRU_HEAD_DIM),
        "b_rg_a": 0.02 * jax.random.normal(ks[10], (L, LRU_WIDTH), jnp.float32),
        "w_rg_x": _normal(ks[12], (L, LRU_HEADS, LRU_HEAD_DIM, LRU_HEAD_DIM), LRU_HEAD_DIM),
        "b_rg_x": 0.02 * jax.random.normal(ks[13], (L, LRU_WIDTH), jnp.float32),
        "lru_lambda": lru_lambda,
        "pool_w": _normal(ks[14], (L, POOL_GROUPS, POOL_GROUP_DIM, POOL_GROUP_DIM), POOL_GROUP_DIM),
        "pool_scale": 1.0 + 0.1 * jax.random.normal(ks[15], (L, POOL_WIDTH), jnp.float32),
        "w_branch_lru": _normal(ks[16], (L, LRU_WIDTH, D_MODEL), LRU_WIDTH),
        "w_branch_pool": _normal(ks[17], (L, POOL_WIDTH, D_MODEL), POOL_WIDTH),
        "w_o": _normal(ks[18], (L, D_MODEL, D_MODEL), D_MODEL),
        "g_ffn_pre": 1.0 + 0.05 * jax.random.normal(ks[19], (L, D_MODEL), jnp.float32),
        "g_ffn_post": 1.0 + 0.05 * jax.random.normal(ks[20], (L, D_MODEL), jnp.float32),
        "w_gate_up": _normal(ks[21], (L, D_MODEL, 2 * D_FF), D_MODEL),
        "w_down": _normal(ks[22], (L, D_FF, D_MODEL), D_FF),
    }


def reference(x, c, w_ada, b_ada, g_mix_pre, g_mix_post, w_in, conv_w, conv_b,
              w_rg_a, b_rg_a, w_rg_x, b_rg_x, lru_lambda, pool_w, pool_scale,
              w_branch_lru, w_branch_pool, w_o, g_ffn_pre, g_ffn_post, w_gate_up, w_down):
    splits = [LRU_WIDTH, 2 * LRU_WIDTH, 2 * LRU_WIDTH + POOL_WIDTH,
              2 * LRU_WIDTH + POOL_WIDTH + D_MODEL]
    c_act = jax.nn.silu(c)
    for l in range(DEPTH):
        mod = c_act @ w_ada[l] + b_ada[l]
        sh_m, sc_m, gt_m, sh_f, sc_f, gt_f = jnp.split(mod, N_MOD, axis=-1)

        u = _modulate(_rmsnorm(x, g_mix_pre[l]), sh_m, sc_m)
        proj = u @ w_in[l]
        xr, gr, xp, m_lru, m_pool = jnp.split(proj, splits, axis=-1)
        xr = _causal_depthwise_conv(xr, conv_w[l], conv_b[l])
        y_lru = _rg_lru(xr, w_rg_a[l], b_rg_a[l], w_rg_x[l], b_rg_x[l], lru_lambda[l]) * jax.nn.gelu(gr)
        y_pool = _multiscale_pool(xp, pool_w[l], pool_scale[l])
        merged = (jax.nn.sigmoid(m_lru) * (y_lru @ w_branch_lru[l])
                  + jax.nn.sigmoid(m_pool) * (y_pool @ w_branch_pool[l]))
        y = merged @ w_o[l]
        x = x + gt_m[:, None, :] * _rmsnorm(y, g_mix_post[l])

        u = _modulate(_rmsnorm(x, g_ffn_pre[l]), sh_f, sc_f)
        gate, up = jnp.split(u @ w_gate_up[l], 2, axis=-1)
        y = (jax.nn.silu(gate) * up) @ w_down[l]
        x = x + gt_f[:, None, :] * _rmsnorm(y, g_ffn_post[l])
    return x
```

```python
import contextlib
import numpy as np
import concourse.bass as bass
import concourse.mybir as mybir
from concourse.bass_utils import run_bass_kernel_spmd

F32 = mybir.dt.float32
BF16 = mybir.dt.bfloat16
AF = mybir.ActivationFunctionType
ALU = mybir.AluOpType

D = 4096
T = 512
NT_MAIN = 4
NT_PRE = 12
DFF = 11008
HC = DFF // 128
NW = 4
EPS = 1e-6
N_CORES = 8
TOK_CORE = 2048

ENGS = ("pe", "act", "dve", "pool", "sp")


class Buf:
    __slots__ = ("name", "w", "r")

    def __init__(self, name=""):
        self.name = name
        self.w = None
        self.r = set()


class Prog:
    def __init__(self, nc, es):
        self.nc = nc
        self.es = es
        self.q = {e: [] for e in ENGS}
        self.epoch = {e: 0 for e in ENGS}
        self.sem = {("tl", e, 0): es.enter_context(nc.semaphore("tl_" + e + "_0")) for e in ENGS}
        self.cnt = {e: 0 for e in ENGS}
        self.pending = {e: False for e in ENGS}
        self.seen = {e: {} for e in ENGS}
        self.dsem = {}
        self.dcnt = {}
        self.sp_out = {}

    def _deps(self, eng, reads, writes, extra):
        best = {}
        seen = self.seen[eng]

        def add(t):
            k, v = t
            if eng == "pe" and k[0] == "tl" and k[1] == "pe":
                return
            if seen.get(k, 0) >= v:
                return
            if best.get(k, 0) < v:
                best[k] = v

        for b in reads:
            if b.w is not None:
                add(b.w)
        for b in writes:
            if b.w is not None:
                add(b.w)
            for t in b.r:
                add(t)
        for t in extra:
            if t is not None:
                add(t)
        for k, v in best.items():
            seen[k] = v
        return list(best.items())

    def _semobj(self, k):
        return self.sem[k] if k[0] == "tl" else self.dsem[k[1]]

    def tlkey(self, e):
        return ("tl", e, self.epoch[e])

    def new_epoch(self):
        for e in ("pe", "act", "dve"):
            assert not self.pending[e]
            self.epoch[e] += 1
            self.sem[self.tlkey(e)] = self.es.enter_context(self.nc.semaphore(f"tl_{e}_{self.epoch[e]}"))
            self.cnt[e] = 0

    def _mark(self, tok, reads, writes):
        for b in reads:
            b.r.add(tok)
        for b in writes:
            b.w = tok
            b.r = set()

    def op(self, eng, fn, reads=(), writes=(), inc=True, extra=()):
        waits = self._deps(eng, reads, writes, extra)
        if inc:
            self.cnt[eng] += 1
            self.pending[eng] = False
            tok = (self.tlkey(eng), self.cnt[eng])
        else:
            self.pending[eng] = True
            tok = (self.tlkey(eng), self.cnt[eng] + 1)
        self.q[eng].append((waits, fn, self.tlkey(eng) if inc else None, 1))
        self._mark(tok, reads, writes)
        return tok

    def dma(self, eng, key, fn, reads=(), writes=(), extra=()):
        if key not in self.dsem:
            self.dsem[key] = self.es.enter_context(self.nc.semaphore("d_" + str(key)))
            self.dcnt[key] = 0
        waits = self._deps(eng, reads, writes, extra)
        self.dcnt[key] += 16
        tok = (("d", key), self.dcnt[key])
        self.q[eng].append((waits, fn, ("d", key), 16))
        self._mark(tok, reads, writes)
        if eng == "sp":
            self.sp_out[key] = tok
        return tok

    def fence(self):
        for e in ("pe", "act", "dve"):
            assert not self.pending[e]
        toks = [(self.tlkey(e), self.cnt[e]) for e in ("pe", "act", "dve") if self.cnt[e] > 0]
        toks += list(self.sp_out.values())
        for e in ("pe", "act", "dve", "sp"):
            waits = self._deps(e, (), (), toks)
            if waits:
                self.q[e].append((waits, None, None, 0))

    def wait_all(self, eng, toks):
        waits = self._deps(eng, (), (), toks)
        self.q[eng].append((waits, None, None, 0))

    def emit(self, block):
        for e in ENGS:
            assert not self.pending[e], e

        def run(engobj, e):
            for waits, fn, inc, amt in self.q[e]:
                for k, v in waits:
                    engobj.wait_ge(self._semobj(k), v)
                if fn is not None:
                    ins = fn(engobj)
                    if inc is not None:
                        ins.then_inc(self._semobj(inc), amt)

        @block.tensor
        def _(t):
            run(t, "pe")

        @block.scalar
        def _(s):
            run(s, "act")

        @block.vector
        def _(v):
            run(v, "dve")

        @block.gpsimd
        def _(g):
            run(g, "pool")

        @block.sync
        def _(s):
            run(s, "sp")


V_GMP, V_GMPOST, V_GFP, V_GFPOST, V_CW0, V_CW1, V_CW2, V_CW3, V_CB, V_BA, V_BX, V_LAM, V_PS, V_C, V_BADA = (
    0, 1, 2, 3, 4, 5, 6, 7, 8, 9, 10, 11, 12, 13, 14)
POOL_W = (2, 4, 8, 16)


def build_nc(n_main=NT_MAIN, n_pre=NT_PRE):
    nc = bass.Bass("TRN2", target_bir_lowering=False)
    dt_in = lambda name, shape: nc.dram_tensor(name, shape, F32, kind="ExternalInput").ap()
    xm = dt_in("xm", [TOK_CORE, D])
    xprev = dt_in("xprev", [NT_PRE * T, D]) if n_pre > 0 else None
    vecs = dt_in("vecs", [640, 128])
    flags = dt_in("flags", [128, 16])
    invc = dt_in("invc", [128, 64])
    ident_d = dt_in("ident", [128, 128])
    win = dt_in("win", [160, 128, 4096])
    wrg = dt_in("wrg", [16, 128, 1024])
    wpool = dt_in("wpool", [8, 128, 4096])
    wbl = dt_in("wbl", [32, 128, 4096])
    wbp = dt_in("wbp", [32, 128, 4096])
    wo = dt_in("wo", [32, 128, 4096])
    wgu = dt_in("wgu", [172, 128, 4096])
    wd = dt_in("wd", [32, 128, DFF])
    wada = dt_in("wada", [192, 128, 4096])
    y = nc.dram_tensor("y", [TOK_CORE, D], F32, kind="ExternalOutput").ap()
    ysc_d = nc.dram_tensor("ysc_d", [T, D], F32).ap()
    x1_d = nc.dram_tensor("x1_d", [T, D], F32).ap()

    with contextlib.ExitStack() as es:
        P = Prog(nc, es)
        sbt = lambda name, shape, dt: es.enter_context(nc.sbuf_tensor(name, shape, dt))
        u = sbt("u", [128, 32, T], BF16)
        bcd = sbt("bcd", [128, 96 * 256], F32)
        scr = sbt("scr", [128, 9248], F32)
        wslot = [sbt(f"wslot{i}", [128, 4096], BF16) for i in range(NW)]
        vecT = sbt("vecT", [128, 640], F32)
        modT = sbt("modT", [128, 192], F32)
        der = sbt("der", [128, 6 * 32], F32)
        nsp = sbt("nsp", [128, 64], F32)
        state = sbt("state", [128, 32], F32)
        xr_tail = sbt("xr_tail", [128, 32 * 3], F32)
        xp_tail = sbt("xp_tail", [128, 32 * 15], F32)
        identf = sbt("identf", [128, 128], F32)
        identb = sbt("identb", [128, 128], BF16)
        onesc = sbt("onesc", [128, 1], F32)
        flg = sbt("flg", [128, 16], F32)
        invt = sbt("invt", [128, 64], F32)
        cact = sbt("cact", [128, 32], BF16)
        small = sbt("small", [128, 32], F32)
        psum = [es.enter_context(nc.psum_tensor(f"ps{i}", [128, 512], F32)) for i in range(8)]
        block = es.enter_context(nc.Block())

        b_u = [Buf(f"u{c}") for c in range(32)]
        b_bcd = [Buf(f"bcd{c}") for c in range(96)]
        b_w = [Buf(f"w{i}") for i in range(NW)]
        b_ps = [Buf(f"ps{i}") for i in range(8)]
        b_vecT, b_modT, b_der, b_nsp = Buf(), Buf(), Buf(), Buf()
        b_state = [Buf() for _ in range(32)]
        b_xrt = [Buf() for _ in range(32)]
        b_xpt = [Buf() for _ in range(32)]
        b_const = Buf()
        b_small = Buf()
        b_small2 = Buf()
        b_ysc_d = [Buf() for _ in range(4)]
        b_x1_d = [Buf() for _ in range(4)]

        def bview(c):
            return bcd[:, c * 256:(c + 1) * 256].bitcast(BF16)

        xblk = [bcd[:, i * 4096:(i + 1) * 4096] for i in range(2)]
        yscp = [bcd[:, 8192 + i * 512: 8192 + (i + 1) * 512] for i in range(2)]
        xnp_ = [bcd[:, 9216 + i * 512: 9216 + (i + 1) * 512].bitcast(BF16) for i in range(2)]
        sqj = bcd[:, 10240:10752].bitcast(BF16)
        b_xblk = [Buf(), Buf()]
        b_yscp = [Buf(), Buf()]
        b_xnp = [Buf(), Buf()]
        b_sqj = Buf()

        ring = [0]
        accn = [0]
        wep = [0]

        def U(k, c):
            return u[:, c, :] if k == 0 else bview(48 + c)

        def UB(k):
            return b_u if k == 0 else b_bcd[48:80]

        def wload(src, ncols=4096):
            s = ring[0] % NW
            ring[0] += 1
            P.dma("pool", f"w{s}_{wep[0]}", lambda e: e.dma_start(out=wslot[s][:, 0:ncols], in_=src), writes=[b_w[s]])
            return s

        def acc():
            b = accn[0] % 4
            accn[0] += 1
            return b

        def mm_group(out_ap, out_buf, pairs, reads):
            n = len(pairs)
            for i, (l, r) in enumerate(pairs):
                P.op("pe", lambda e, l=l, r=r, i=i: e.matmul(out_ap, lhsT=l, rhs=r, start=(i == 0), stop=(i == n - 1)),
                     reads=reads, writes=[out_buf], inc=(i == n - 1))

        def act(out, in_, func, reads, writes, bias=None, scale=None, accum=None):
            kw = {}
            if bias is not None:
                kw["bias"] = bias
            if scale is not None:
                kw["scale"] = scale
            if accum is not None:
                kw["accum_out"] = accum
            return P.op("act", lambda e: e.activation(out=out, in_=in_, func=func, **kw), reads=reads, writes=writes)

        def tt(out, in0, in1, op, reads, writes, eng="dve"):
            return P.op(eng, lambda e: e.tensor_tensor(out=out, in0=in0, in1=in1, op=op), reads=reads, writes=writes)

        def ts(out, in0, s1, s2, op0, op1, reads, writes):
            if s2 is None:
                return P.op("dve", lambda e: e.tensor_scalar(out=out, in0=in0, scalar1=s1, scalar2=None, op0=op0), reads=reads, writes=writes)
            return P.op("dve", lambda e: e.tensor_scalar(out=out, in0=in0, scalar1=s1, scalar2=s2, op0=op0, op1=op1), reads=reads, writes=writes)

        def stt(out, in0, scalar, in1, op0, op1, reads, writes):
            return P.op("dve", lambda e: e.scalar_tensor_tensor(out=out, in0=in0, scalar=scalar, in1=in1, op0=op0, op1=op1), reads=reads, writes=writes)

        def cp(out, in_, reads, writes):
            return P.op("dve", lambda e: e.tensor_copy(out=out, in_=in_), reads=reads, writes=writes)

        def V(i):
            return vecT[:, i * 32:(i + 1) * 32]

        def Vc(i, c):
            return vecT[:, i * 32 + c: i * 32 + c + 1]

        P.dma("sp", "c0", lambda e: e.dma_start(out=identf[:], in_=ident_d), writes=[b_const])
        P.dma("sp", "c1", lambda e: e.dma_start(out=flg[:], in_=flags), writes=[b_const])
        P.dma("sp", "c2", lambda e: e.dma_start(out=invt[:], in_=invc), writes=[b_const])
        cp(identb[:], identf[:], [b_const], [b_const])
        P.op("dve", lambda e: e.memset(onesc[:], 1.0), writes=[b_const])
        P.op("dve", lambda e: e.memset(state[:], 0.0), writes=b_state)
        P.op("dve", lambda e: e.memset(xr_tail[:], 0.0), writes=b_xrt)
        P.op("dve", lambda e: e.memset(xp_tail[:], 0.0), writes=b_xpt)
        b_vst = [Buf() for _ in range(5)]
        for i in range(5):
            P.dma("sp", f"v{i}", lambda e, i=i: e.dma_start(out=scr[:, i * 128:(i + 1) * 128], in_=vecs[i * 128:(i + 1) * 128, :]), writes=[b_vst[i]])
        for i in range(5):
            pb = 4 + (i % 2)
            P.op("pe", lambda e, i=i, pb=pb: e.transpose(out=psum[pb][:, 0:128], in_=scr[:, i * 128:(i + 1) * 128], identity=identf[:]),
                 reads=[b_vst[i], b_const], writes=[b_ps[pb]])
            cp(vecT[:, i * 128:(i + 1) * 128], psum[pb][:, 0:128], [b_ps[pb]], [b_vecT])
        act(cact[:], V(V_C), AF.Silu, [b_vecT], [b_const])
        act(nsp[:, 0:32], V(V_LAM), AF.Exp, [b_vecT], [b_nsp], scale=-1.0)
        act(nsp[:, 0:32], nsp[:, 0:32], AF.Ln, [b_nsp], [b_nsp], bias=1.0)
        ts(nsp[:, 32:64], nsp[:, 0:32], -16.0, None, ALU.mult, None, [b_nsp], [b_nsp])
        ts(nsp[:, 0:32], nsp[:, 0:32], -8.0, None, ALU.mult, None, [b_nsp], [b_nsp])

        b_row = [Buf(), Buf()]
        rowsb = [scr[0:1, 1024 + i * 512: 1024 + (i + 1) * 512] for i in range(2)]
        rowps = [7, 5]
        for g in range(48):
            rb = rowps[g % 2]
            for q in range(4):
                s = wload(wada[g * 4 + q])
                for kk in range(8):
                    kc = q * 8 + kk
                    P.op("pe", lambda e, s=s, kk=kk, kc=kc, rb=rb: e.matmul(psum[rb][0:1, :], lhsT=cact[:, kc:kc + 1], rhs=wslot[s][:, kk * 512:(kk + 1) * 512],
                                                                          start=(kc == 0), stop=(kc == 31)),
                         reads=[b_w[s], b_const], writes=[b_ps[rb]], inc=(kc == 31))
            act(rowsb[g % 2], psum[rb][0:1, :], AF.Copy, [b_ps[rb]], [b_row[g % 2]])
            for q in range(4):
                P.op("pe", lambda e, g=g, q=q: e.matmul(psum[6][:, g * 4 + q: g * 4 + q + 1], lhsT=rowsb[g % 2][0:1, q * 128:(q + 1) * 128], rhs=identf[0:1, 0:1],
                                                       start=True, stop=True),
                     reads=[b_row[g % 2], b_const], writes=[b_ps[6]])
        tt(modT[:], psum[6][:, 0:192], vecT[:, 448:640], ALU.add, [b_ps[6], b_vecT], [b_modT])
        GM_M, SH_M, GTG_M, GM_F, SH_F, GTG_F = range(6)

        def DER(i, c):
            return der[:, i * 32 + c: i * 32 + c + 1]

        stt(der[:, 0:32], modT[:, 32:64], 1.0, V(V_GMP), ALU.add, ALU.mult, [b_modT, b_vecT], [b_der])
        cp(der[:, 32:64], modT[:, 0:32], [b_modT], [b_der])
        tt(der[:, 64:96], modT[:, 64:96], V(V_GMPOST), ALU.mult, [b_modT, b_vecT], [b_der])
        stt(der[:, 96:128], modT[:, 128:160], 1.0, V(V_GFP), ALU.add, ALU.mult, [b_modT, b_vecT], [b_der])
        cp(der[:, 128:160], modT[:, 96:128], [b_modT], [b_der])
        tt(der[:, 160:192], modT[:, 160:192], V(V_GFPOST), ALU.mult, [b_modT, b_vecT], [b_der])

        def prologue_block(i, tb, gm_i, sh_i, k=0):
            xb = xblk[i]
            P.op("dve", lambda e: e.memset(small[:, 0:4], 0.0), writes=[b_small])
            for q in range(4):
                act(sqj, xb[:, q * 1024:(q + 1) * 1024], AF.Square, [b_xblk[i], b_small], [b_sqj, b_small], accum=small[:, q:q + 1])
            P.op("dve", lambda e: e.reduce_sum(out=small[:, 4:5], in_=small[:, 0:4], axis=mybir.AxisListType.X), reads=[b_small], writes=[b_small])
            act(small[:, 5:6], small[:, 4:5], AF.Sqrt, [b_small], [b_small], bias=EPS, scale=1.0 / D)
            P.op("dve", lambda e: e.reciprocal(out=small[:, 6:7], in_=small[:, 5:6]), reads=[b_small], writes=[b_small])
            for q in range(4):
                xi = q % 2
                act(xnp_[xi], xb[:, q * 1024:(q + 1) * 1024], AF.Copy, [b_xblk[i], b_small], [b_xnp[xi]], scale=small[:, 6:7])
                pb = 4 + xi
                pst = psum[pb][:, 0:512].bitcast(BF16)
                for j in range(8):
                    P.op("pe", lambda e, j=j, xi=xi, pst=pst: e.transpose(out=pst[:, j * 128:(j + 1) * 128], in_=xnp_[xi][:, j * 128:(j + 1) * 128], identity=identb[:]),
                         reads=[b_xnp[xi], b_const], writes=[b_ps[pb]])
                for j in range(8):
                    c = q * 8 + j
                    o = U(k, c)[:, tb * 128:(tb + 1) * 128]
                    src = pst[:, j * 128:(j + 1) * 128]
                    if pb == 4:
                        ts(o, src, DER(gm_i, c), DER(sh_i, c), ALU.mult, ALU.add, [b_ps[pb], b_der], [UB(k)[c]])
                    else:
                        act(o, src, AF.Identity, [b_ps[pb], b_der], [UB(k)[c]], bias=DER(sh_i, c), scale=DER(gm_i, c))

        def prologue_from_dram(rows_ap_fn, gm_i, sh_i):
            for tb in range(4):
                i = tb % 2
                P.dma("sp", f"xb{i}", lambda e, i=i, tb=tb: e.dma_start(out=xblk[i], in_=rows_ap_fn(tb)), writes=[b_xblk[i]])
                prologue_block(i, tb, gm_i, sh_i)

        XR = lambda k: scr[:, k * 520: k * 520 + 515]
        XC = lambda k: scr[:, 2080 + k * 512: 2080 + (k + 1) * 512]
        XCB = lambda k: scr[:, 4128 + k * 256: 4128 + (k + 1) * 256].bitcast(BF16)
        TMP = lambda s_, k: scr[:, 5152 + (s_ * 4 + k) * 512: 5152 + (s_ * 4 + k + 1) * 512]
        b_xr = [Buf() for _ in range(4)]
        b_xc = [Buf() for _ in range(4)]
        b_xcb = [Buf() for _ in range(4)]
        b_tmp = [[Buf() for _ in range(4)] for _ in range(2)]

        def lru_stage_a(hd, flag_col, ku=0):
            for jj in range(2):
                j = 2 * hd + jj
                k = (hd % 2) * 2 + jj
                s = wload(win[j])
                b = acc()
                mm_group(psum[b][:, :], b_ps[b], [(wslot[s][:, kc * 128:(kc + 1) * 128], U(ku, kc)) for kc in range(32)], [b_w[s]] + UB(ku))
                xr = XR(k)
                act(xr[:, 3:515], psum[b][:, :], AF.Copy, [b_ps[b]], [b_xr[k]])
                cp(xr[:, 0:3], xr_tail[:, j * 3:(j + 1) * 3], [b_xrt[j]], [b_xr[k]])
                xc = XC(k)
                ts(xc, xr[:, 3:515], Vc(V_CW3, j), Vc(V_CB, j), ALU.mult, ALU.add, [b_xr[k], b_vecT], [b_xc[k]])
                for kk in (2, 1, 0):
                    stt(xc, xr[:, kk:kk + 512], Vc(V_CW0 + kk, j), xc, ALU.mult, ALU.add, [b_xr[k], b_vecT, b_xc[k]], [b_xc[k]])
                if flag_col is None:
                    cp(xr_tail[:, j * 3:(j + 1) * 3], xr[:, 512:515], [b_xr[k]], [b_xrt[j]])
                else:
                    ts(xr_tail[:, j * 3:(j + 1) * 3], xr[:, 512:515], flg[:, flag_col:flag_col + 1], None, ALU.mult, None, [b_xr[k], b_const], [b_xrt[j]])
                act(XCB(k), xc, AF.Copy, [b_xc[k]], [b_xcb[k]])

        def lru_stage_b(hd, flag_col, main):
            s = wload(wrg[hd], 1024)
            kbase = (hd % 2) * 2
            sg = None
            if main:
                sg = [wload(win[32 + 2 * hd + jo]) for jo in range(2)]
            for jo in range(2):
                j = 2 * hd + jo
                R, I, A_, M = (TMP(jo, 0), TMP(jo, 1), TMP(jo, 2), TMP(jo, 3))
                bR, bI, bA, bM = b_tmp[jo]
                for gate in range(2):
                    b = acc()
                    mm_group(psum[b][:, :], b_ps[b],
                             [(wslot[s][:, ((gate * 2 + ki) * 2 + jo) * 128:((gate * 2 + ki) * 2 + jo + 1) * 128], XCB(kbase + ki)) for ki in range(2)],
                             [b_w[s], b_xcb[kbase], b_xcb[kbase + 1]])
                    if gate == 0:
                        act(R, psum[b][:, :], AF.Sigmoid, [b_ps[b], b_vecT], [bR], bias=Vc(V_BA, j))
                    else:
                        act(I, psum[b][:, :], AF.Sigmoid, [b_ps[b], b_vecT], [bI], bias=Vc(V_BX, j))
                act(A_, R, AF.Exp, [bR, b_nsp], [bA], scale=nsp[:, j:j + 1])
                act(M, R, AF.Exp, [bR, b_nsp], [bM], scale=nsp[:, 32 + j:33 + j])
                act(M, M, AF.Sqrt, [bM], [bM], bias=1.0, scale=-1.0)
                tt(I, I, XC(kbase + jo), ALU.mult, [bI, b_xc[kbase + jo]], [bI])
                tt(I, I, M, ALU.mult, [bI, bM], [bI])
                P.op("dve", lambda e, M=M, A_=A_, I=I, j=j: e.tensor_tensor_scan(out=M, data0=A_, data1=I, initial=state[:, j:j + 1], op0=ALU.mult, op1=ALU.add),
                     reads=[bA, bI, b_state[j], bM], writes=[bM])
                if flag_col is None:
                    cp(state[:, j:j + 1], M[:, 511:512], [bM], [b_state[j]])
                else:
                    ts(state[:, j:j + 1], M[:, 511:512], flg[:, flag_col:flag_col + 1], None, ALU.mult, None, [bM, b_const], [b_state[j]])
                if main:
                    b = acc()
                    s2 = sg[jo]
                    mm_group(psum[b][:, :], b_ps[b], [(wslot[s2][:, kc * 128:(kc + 1) * 128], u[:, kc, :]) for kc in range(32)], [b_w[s2]] + b_u)
                    act(R, psum[b][:, :], AF.Gelu_apprx_tanh, [b_ps[b], bR], [bR])
                    tt(bview(j), M, R, ALU.mult, [bM, bR], [b_bcd[j]])

        def lru_branch(flag_col, main, ku=0, hook=None):
            lru_stage_a(0, flag_col, ku)
            for hd in range(16):
                if hd + 1 < 16:
                    lru_stage_a(hd + 1, flag_col, ku)
                lru_stage_b(hd, flag_col, main)
                if hook is not None:
                    hook(hd)

        XP = lambda k: scr[:, k * 528: k * 528 + 527]
        TA = lambda k, w_: scr[:, 1056 + (2 * k + w_) * 528: 1056 + (2 * k + w_) * 528 + 527]
        PO = lambda k: scr[:, 3168 + k * 256: 3168 + (k + 1) * 256].bitcast(BF16)
        T16 = lambda k: scr[:, 5216 + k * 16: 5216 + (k + 1) * 16]
        b_xp = [Buf(), Buf()]
        b_ta = [[Buf(), Buf()], [Buf(), Buf()]]
        b_po = [Buf() for _ in range(8)]
        b_t16 = [Buf(), Buf()]

        def pool_branch(first_tile):
            cnt = 0
            for g in range(4):
                w_ = POOL_W[g]
                for k in range(8):
                    c = 8 * g + k
                    i = cnt % 2
                    cnt += 1
                    s = wload(win[64 + c])
                    b = acc()
                    mm_group(psum[b][:, :], b_ps[b], [(wslot[s][:, kc * 128:(kc + 1) * 128], u[:, kc, :]) for kc in range(32)], [b_w[s]] + b_u)
                    xp = XP(i)
                    act(xp[:, 15:527], psum[b][:, :], AF.Copy, [b_ps[b]], [b_xp[i]])
                    cp(xp[:, 0:15], xp_tail[:, c * 15:(c + 1) * 15], [b_xpt[c]], [b_xp[i]])
                    cur, cur_b, lo, step, wi = xp, b_xp[i], 1, 1, 0
                    for lvl in range(g + 1):
                        dst, dst_b = TA(i, wi), b_ta[i][wi]
                        tt(dst[:, lo:527], cur[:, lo:527], cur[:, lo - step:527 - step], ALU.add, [cur_b], [dst_b])
                        cur, cur_b = dst, dst_b
                        step *= 2
                        lo = 2 * step - 1
                        wi ^= 1
                    stt(PO(k), cur[:, 15:527], 1.0 / w_, xp[:, 15:527], ALU.mult, ALU.subtract, [cur_b, b_xp[i]], [b_po[k]])
                    if first_tile:
                        tt(T16(i), cur[:, 15:31], invt[:, g * 16:(g + 1) * 16], ALU.mult, [cur_b, b_const], [b_t16[i]])
                        tt(PO(k)[:, 0:16], T16(i), xp[:, 15:31], ALU.subtract, [b_t16[i], b_xp[i]], [b_po[k]])
                    act(xp_tail[:, c * 15:(c + 1) * 15], xp[:, 512:527], AF.Copy, [b_xp[i]], [b_xpt[c]])
                for nh in range(2):
                    s = wload(wpool[g * 2 + nh])
                    for n4 in range(4):
                        n = 8 * g + nh * 4 + n4
                        b = acc()
                        mm_group(psum[b][:, :], b_ps[b],
                                 [(wslot[s][:, kc * 512 + n4 * 128: kc * 512 + (n4 + 1) * 128], PO(kc)) for kc in range(8)],
                                 [b_w[s]] + b_po)
                        act(bview(32 + n), psum[b][:, :], AF.Copy, [b_ps[b], b_vecT], [b_bcd[32 + n]], scale=Vc(V_PS, n))

        S1 = lambda i, k: scr[:, (k * 2 + i) * 512: (k * 2 + i + 1) * 512]
        b_m3 = [[Buf() for _ in range(4)] for _ in range(2)]

        def merge_phase():
            rhs_u = [u[:, kc, :] for kc in range(32)]
            for n in range(32):
                i = n % 2
                s1, t1, s3, t2 = (S1(i, k) for k in range(4))
                bs1, bt1, bs3, bt2 = b_m3[i]
                for br in range(2):
                    sw = wload(win[96 + 32 * br + n])
                    b = acc()
                    mm_group(psum[b][:, :], b_ps[b], [(wslot[sw][:, kc * 128:(kc + 1) * 128], rhs_u[kc]) for kc in range(32)], [b_w[sw]] + b_u)
                    act(s1 if br == 0 else s3, psum[b][:, :], AF.Sigmoid, [b_ps[b]], [bs1 if br == 0 else bs3])
                    sw = wload((wbl if br == 0 else wbp)[n])
                    b = acc()
                    yb = b_bcd[32 * br:32 * br + 32]
                    mm_group(psum[b][:, :], b_ps[b], [(wslot[sw][:, kc * 128:(kc + 1) * 128], bview(32 * br + kc)) for kc in range(32)], [b_w[sw]] + yb)
                    if br == 0:
                        tt(t1, s1, psum[b][:, :], ALU.mult, [bs1, b_ps[b]], [bt1])
                    else:
                        tt(t2, s3, psum[b][:, :], ALU.mult, [bs3, b_ps[b]], [bt2])
                tt(bview(64 + n), t1, t2, ALU.add, [bt1, bt2], [b_bcd[64 + n]])

        YSC = lambda i: scr[:, 4096 + i * 512: 4096 + (i + 1) * 512]
        SQ = lambda i: scr[:, 5120 + i * 512: 5120 + (i + 1) * 512]
        YTK = lambda i: scr[:, 6144 + i * 512: 6144 + (i + 1) * 512]
        b_ysc = [Buf(), Buf()]
        b_sq = [Buf(), Buf()]
        b_ytk = [Buf(), Buf()]
        ysc_d_v = ysc_d.rearrange("(tb p) f -> p tb f", p=128)
        ysc_toks = [None, None]

        def epi_chunk(n, b, gtg_i):
            i = n % 2
            act(YSC(i), psum[b][:, :], AF.Copy, [b_ps[b], b_der], [b_ysc[i]], scale=DER(gtg_i, n))
            act(SQ(i), psum[b][:, :], AF.Square, [b_ps[b]], [b_sq[i]])
            P.op("pe", lambda e, i=i, n=n: e.matmul(psum[6][0:1, :], lhsT=onesc[:, 0:1], rhs=SQ(i), start=(n == 0), stop=(n == 31)),
                 reads=[b_sq[i], b_const], writes=[b_ps[6]])
            pb = 4 + i
            for tb in range(4):
                P.op("pe", lambda e, i=i, tb=tb, pb=pb: e.transpose(out=psum[pb][:, tb * 128:(tb + 1) * 128], in_=YSC(i)[:, tb * 128:(tb + 1) * 128], identity=identf[:]),
                     reads=[b_ysc[i], b_const], writes=[b_ps[pb]])
            cp(YTK(i), psum[pb][:, :], [b_ps[pb]], [b_ytk[i]])
            ysc_toks[i] = P.dma("sp", f"yk{i}", lambda e, i=i, n=n: e.dma_start(out=ysc_d_v[:, :, n * 128:(n + 1) * 128], in_=YTK(i).rearrange("p (tb f) -> p tb f", tb=4)),
                  reads=[b_ytk[i]])

        def epi_rstd():
            row = scr[0:1, 7168:7680]
            b_r = Buf()
            act(row, psum[6][0:1, :], AF.Sqrt, [b_ps[6]], [b_r], bias=EPS, scale=1.0 / D)
            P.op("dve", lambda e: e.reciprocal(out=row, in_=row), reads=[b_r], writes=[b_r])
            for tb in range(4):
                P.op("pe", lambda e, tb=tb: e.matmul(psum[7][:, tb:tb + 1], lhsT=row[0:1, tb * 128:(tb + 1) * 128], rhs=identf[0:1, 0:1], start=True, stop=True),
                     reads=[b_r, b_const], writes=[b_ps[7]])
            cp(small[:, 12:16], psum[7][:, 0:4], [b_ps[7]], [b_small2])

        def pass2_block(tb, i, xsrc_ap, xsrc_bufs):
            P.dma("sp", f"xb{i}", lambda e: e.dma_start(out=xblk[i], in_=xsrc_ap), reads=xsrc_bufs, writes=[b_xblk[i]])
            for q in range(8):
                yi = q % 2
                P.dma("sp", f"yp{yi}", lambda e, yi=yi, q=q: e.dma_start(out=yscp[yi], in_=ysc_d[tb * 128:(tb + 1) * 128, q * 512:(q + 1) * 512]),
                      writes=[b_yscp[yi]], extra=list(ysc_toks))
                stt(xblk[i][:, q * 512:(q + 1) * 512], yscp[yi], small[:, 12 + tb:13 + tb], xblk[i][:, q * 512:(q + 1) * 512], ALU.mult, ALU.add,
                    [b_yscp[yi], b_small2, b_xblk[i]], [b_xblk[i]])

        pts = list(range(NT_PRE - n_pre, NT_PRE))

        def pre_load(pt, tb):
            i = tb % 2
            P.dma("sp", f"xb{i}", lambda e: e.dma_start(out=xblk[i], in_=xprev[pt * T + tb * 128: pt * T + (tb + 1) * 128, :]), writes=[b_xblk[i]])

        if pts:
            P.fence()
            for tb in range(4):
                pre_load(pts[0], tb)
                prologue_block(tb % 2, tb, GM_M, SH_M, 0)
        for idx, pt in enumerate(pts):
            ku = idx % 2
            nxt = pts[idx + 1] if idx + 1 < len(pts) else None

            def hook(hd, nxt=nxt, ku=ku):
                if nxt is None:
                    return
                if hd == 0:
                    pre_load(nxt, 0)
                if hd in (2, 6, 10, 14):
                    tb = (hd - 2) // 4
                    if tb + 1 < 4:
                        pre_load(nxt, tb + 1)
                    prologue_block(tb % 2, tb, GM_M, SH_M, 1 - ku)

            lru_branch(pt, False, ku, hook)
            if pt == NT_PRE - 1:
                for c in range(32):
                    s = wload(win[64 + c])
                    hb = (7, 5)[c % 2]
                    hr = psum[hb][:, 0:16]
                    mm_group(hr, b_ps[hb], [(wslot[s][:, kc * 128:(kc + 1) * 128], U(ku, kc)[:, 496:512]) for kc in range(32)], [b_w[s]] + UB(ku))
                    ts(xp_tail[:, c * 15:(c + 1) * 15], hr[:, 1:16], flg[:, pt:pt + 1], None, ALU.mult, None, [b_ps[hb], b_const], [b_xpt[c]])

        out_toks = []
        for t in range(n_main):
            P.fence()
            if t % 4 == 0:
                P.new_epoch()
                wep[0] = 1 + t // 4
            prologue_from_dram(lambda tb, t=t: xm[t * T + tb * 128: t * T + (tb + 1) * 128, :], GM_M, SH_M)
            P.fence()
            lru_branch(None, True)
            P.fence()
            pool_branch(t == 0)
            P.fence()
            merge_phase()
            for n in range(32):
                s = wload(wo[n])
                b = acc()
                mm_group(psum[b][:, :], b_ps[b], [(wslot[s][:, kc * 128:(kc + 1) * 128], bview(64 + kc)) for kc in range(32)], [b_w[s]] + b_bcd[64:96])
                epi_chunk(n, b, GTG_M)
            epi_rstd()
            P.fence()
            for tb in range(4):
                i = tb % 2
                pass2_block(tb, i, xm[t * T + tb * 128: t * T + (tb + 1) * 128, :], [])
                P.dma("sp", f"xs{i}", lambda e, i=i, tb=tb: e.dma_start(out=x1_d[tb * 128:(tb + 1) * 128, :], in_=xblk[i]), reads=[b_xblk[i]], writes=[b_x1_d[tb]])
                prologue_block(i, tb, GM_F, SH_F)
            P.fence()
            b_sg = [Buf(), Buf()]
            rhs_u = [u[:, kc, :] for kc in range(32)]
            for j in range(HC):
                i = j % 2
                sG = wload(wgu[j])
                bG = acc()
                mm_group(psum[bG][:, :], b_ps[bG], [(wslot[sG][:, kc * 128:(kc + 1) * 128], rhs_u[kc]) for kc in range(32)], [b_w[sG]] + b_u)
                sgv = scr[:, i * 512:(i + 1) * 512]
                act(sgv, psum[bG][:, :], AF.Silu, [b_ps[bG]], [b_sg[i]])
                sU = wload(wgu[HC + j])
                bU = acc()
                mm_group(psum[bU][:, :], b_ps[bU], [(wslot[sU][:, kc * 128:(kc + 1) * 128], rhs_u[kc]) for kc in range(32)], [b_w[sU]] + b_u)
                tt(bview(j), sgv, psum[bU][:, :], ALU.mult, [b_sg[i], b_ps[bU]], [b_bcd[j]])
            for n in range(32):
                ss = [wload(wd[n][:, 0:4096]), wload(wd[n][:, 4096:8192]), wload(wd[n][:, 8192:DFF], DFF - 8192)]
                b = acc()
                mm_group(psum[b][:, :], b_ps[b],
                         [(wslot[ss[kc // 32]][:, (kc % 32) * 128:(kc % 32 + 1) * 128], bview(kc)) for kc in range(HC)],
                         [b_w[s_] for s_ in ss] + b_bcd[0:HC])
                epi_chunk(n, b, GTG_F)
            epi_rstd()
            P.fence()
            for tb in range(4):
                i = tb % 2
                pass2_block(tb, i, x1_d[tb * 128:(tb + 1) * 128, :], [b_x1_d[tb]])
                out_toks.append(P.dma("sp", f"xs{i}", lambda e, i=i, tb=tb, t=t: e.dma_start(out=y[t * T + tb * 128: t * T + (tb + 1) * 128, :], in_=xblk[i]),
                                      reads=[b_xblk[i]]))
        P.fence()
        P.wait_all("sp", out_toks)
        P.emit(block)
    return nc


def _tile_cols(w):
    K, N = w.shape
    return np.ascontiguousarray(w.reshape(K // 128, 128, N // 128, 128).transpose(2, 1, 0, 3)).reshape(N // 128, 128, K)


def _prep_shared(inp):
    sh = {}
    sh["win"] = _tile_cols(inp["w_in"][0])
    wa = inp["w_rg_a"][0].reshape(16, 2, 128, 2, 128).transpose(0, 2, 1, 3, 4)
    wx = inp["w_rg_x"][0].reshape(16, 2, 128, 2, 128).transpose(0, 2, 1, 3, 4)
    sh["wrg"] = np.ascontiguousarray(np.stack([wa, wx], axis=2)).reshape(16, 128, 1024)
    sh["wpool"] = np.ascontiguousarray(inp["pool_w"][0].reshape(4, 8, 128, 2, 512).transpose(0, 3, 2, 1, 4)).reshape(8, 128, 4096)
    sh["wbl"] = _tile_cols(inp["w_branch_lru"][0])
    sh["wbp"] = _tile_cols(inp["w_branch_pool"][0])
    sh["wo"] = _tile_cols(inp["w_o"][0])
    sh["wgu"] = _tile_cols(inp["w_gate_up"][0])
    sh["wd"] = _tile_cols(inp["w_down"][0])
    sh["wada"] = np.ascontiguousarray(inp["w_ada"][0].reshape(4, 8, 128, 48, 512).transpose(3, 0, 2, 1, 4)).reshape(192, 128, 4096)
    sh["ident"] = np.eye(128, dtype=np.float32)
    return sh


def _vec_rows(inp, b):
    rows = [inp["g_mix_pre"][0], inp["g_mix_post"][0], inp["g_ffn_pre"][0], inp["g_ffn_post"][0],
            inp["conv_w"][0, 0], inp["conv_w"][0, 1], inp["conv_w"][0, 2], inp["conv_w"][0, 3],
            inp["conv_b"][0], inp["b_rg_a"][0], inp["b_rg_x"][0], inp["lru_lambda"][0], inp["pool_scale"][0],
            inp["c"][b]]
    v = np.concatenate([np.asarray(r, np.float32).reshape(32, 128) for r in rows] + [np.asarray(inp["b_ada"][0], np.float32).reshape(192, 128)], axis=0)
    return np.ascontiguousarray(v)


def _core_inputs(inp, sh, c):
    b, j = divmod(c, 4)
    x = inp["x"]
    m = dict(sh)
    m["xm"] = np.ascontiguousarray(x[b, j * TOK_CORE:(j + 1) * TOK_CORE, :])
    xp = np.zeros((NT_PRE * T, D), np.float32)
    start = j * TOK_CORE - NT_PRE * T
    if j > 0:
        xp[-j * TOK_CORE:, :] = x[b, 0:j * TOK_CORE, :]
    m["xprev"] = xp
    fl = np.zeros((128, 16), np.float32)
    for pt in range(NT_PRE):
        if start + pt * T >= 0:
            fl[:, pt] = 1.0
    m["flags"] = fl
    iv = np.zeros((128, 64), np.float32)
    for g, w_ in enumerate(POOL_W):
        for t in range(16):
            iv[:, g * 16 + t] = 1.0 / (min(t + 1, w_) if j == 0 else w_)
    m["invc"] = iv
    m["vecs"] = _vec_rows(inp, b)
    return m


_NC_CACHE = {}


def kernel(**inputs):
    inp = {k: np.asarray(v, dtype=np.float32) for k, v in inputs.items()}
    sh = _prep_shared(inp)
    in_maps = [_core_inputs(inp, sh, c) for c in range(N_CORES)]
    if "nc" not in _NC_CACHE:
        _NC_CACHE["nc"] = build_nc()
    nc = _NC_CACHE["nc"]
    res = run_bass_kernel_spmd(nc, in_maps, core_ids=list(range(N_CORES)))
    out = np.empty((2, 8192, D), np.float32)
    for c in range(N_CORES):
        b, j = divmod(c, 4)
        out[b, j * TOK_CORE:(j + 1) * TOK_CORE, :] = res.results[c]["y"]
    return out
```

```python
import contextlib
import numpy as np
import concourse.bass as bass
import concourse.mybir as mybir
from concourse.bass_utils import run_bass_kernel_spmd

F32 = mybir.dt.float32
BF16 = mybir.dt.bfloat16
AF = mybir.ActivationFunctionType
ALU = mybir.AluOpType

D = 4096
T = 512
NT_MAIN = 4
NT_PRE = 12
DFF = 11008
HC = DFF // 128
NW = 4
EPS = 1e-6
N_CORES = 8
TOK_CORE = 2048

ENGS = ("pe", "act", "dve", "pool", "sp")


class Buf:
    __slots__ = ("name", "w", "r")

    def __init__(self, name=""):
        self.name = name
        self.w = None
        self.r = set()


class Prog:
    def __init__(self, nc, es):
        self.nc = nc
        self.es = es
        self.q = {e: [] for e in ENGS}
        self.epoch = {e: 0 for e in ENGS}
        self.sem = {("tl", e, 0): es.enter_context(nc.semaphore("tl_" + e + "_0")) for e in ENGS}
        self.cnt = {e: 0 for e in ENGS}
        self.pending = {e: False for e in ENGS}
        self.seen = {e: {} for e in ENGS}
        self.dsem = {}
        self.dcnt = {}
        self.sp_out = {}

    def _deps(self, eng, reads, writes, extra):
        best = {}
        seen = self.seen[eng]

        def add(t):
            k, v = t
            if eng == "pe" and k[0] == "tl" and k[1] == "pe":
                return
            if seen.get(k, 0) >= v:
                return
            if best.get(k, 0) < v:
                best[k] = v

        for b in reads:
            if b.w is not None:
                add(b.w)
        for b in writes:
            if b.w is not None:
                add(b.w)
            for t in b.r:
                add(t)
        for t in extra:
            if t is not None:
                add(t)
        for k, v in best.items():
            seen[k] = v
        return list(best.items())

    def _semobj(self, k):
        return self.sem[k] if k[0] == "tl" else self.dsem[k[1]]

    def tlkey(self, e):
        return ("tl", e, self.epoch[e])

    def new_epoch(self):
        for e in ("pe", "act", "dve"):
            assert not self.pending[e]
            self.epoch[e] += 1
            self.sem[self.tlkey(e)] = self.es.enter_context(self.nc.semaphore(f"tl_{e}_{self.epoch[e]}"))
            self.cnt[e] = 0

    def _mark(self, tok, reads, writes):
        for b in reads:
            b.r.add(tok)
        for b in writes:
            b.w = tok
            b.r = set()

    def op(self, eng, fn, reads=(), writes=(), inc=True, extra=()):
        waits = self._deps(eng, reads, writes, extra)
        if inc:
            self.cnt[eng] += 1
            self.pending[eng] = False
            tok = (self.tlkey(eng), self.cnt[eng])
        else:
            self.pending[eng] = True
            tok = (self.tlkey(eng), self.cnt[eng] + 1)
        self.q[eng].append((waits, fn, self.tlkey(eng) if inc else None, 1))
        self._mark(tok, reads, writes)
        return tok

    def dma(self, eng, key, fn, reads=(), writes=(), extra=()):
        if key not in self.dsem:
            self.dsem[key] = self.es.enter_context(self.nc.semaphore("d_" + str(key)))
            self.dcnt[key] = 0
        waits = self._deps(eng, reads, writes, extra)
        self.dcnt[key] += 16
        tok = (("d", key), self.dcnt[key])
        self.q[eng].append((waits, fn, ("d", key), 16))
        self._mark(tok, reads, writes)
        if eng == "sp":
            self.sp_out[key] = tok
        return tok

    def fence(self):
        for e in ("pe", "act", "dve"):
            assert not self.pending[e]
        toks = [(self.tlkey(e), self.cnt[e]) for e in ("pe", "act", "dve") if self.cnt[e] > 0]
        toks += list(self.sp_out.values())
        for e in ("pe", "act", "dve", "sp"):
            waits = self._deps(e, (), (), toks)
            if waits:
                self.q[e].append((waits, None, None, 0))

    def wait_all(self, eng, toks):
        waits = self._deps(eng, (), (), toks)
        self.q[eng].append((waits, None, None, 0))

    def emit(self, block):
        for e in ENGS:
            assert not self.pending[e], e

        def run(engobj, e):
            for waits, fn, inc, amt in self.q[e]:
                for k, v in waits:
                    engobj.wait_ge(self._semobj(k), v)
                if fn is not None:
                    ins = fn(engobj)
                    if inc is not None:
                        ins.then_inc(self._semobj(inc), amt)

        @block.tensor
        def _(t):
            run(t, "pe")

        @block.scalar
        def _(s):
            run(s, "act")

        @block.vector
        def _(v):
            run(v, "dve")

        @block.gpsimd
        def _(g):
            run(g, "pool")

        @block.sync
        def _(s):
            run(s, "sp")


V_GMP, V_GMPOST, V_GFP, V_GFPOST, V_CW0, V_CW1, V_CW2, V_CW3, V_CB, V_BA, V_BX, V_LAM, V_PS, V_C, V_BADA = (
    0, 1, 2, 3, 4, 5, 6, 7, 8, 9, 10, 11, 12, 13, 14)
POOL_W = (2, 4, 8, 16)


def build_nc(n_main=NT_MAIN, n_pre=NT_PRE):
    nc = bass.Bass("TRN2", target_bir_lowering=False)
    dt_in = lambda name, shape: nc.dram_tensor(name, shape, F32, kind="ExternalInput").ap()
    xm = dt_in("xm", [TOK_CORE, D])
    xprev = dt_in("xprev", [NT_PRE * T, D]) if n_pre > 0 else None
    vecs = dt_in("vecs", [640, 128])
    flags = dt_in("flags", [128, 16])
    invc = dt_in("invc", [128, 64])
    ident_d = dt_in("ident", [128, 128])
    win = dt_in("win", [160, 128, 4096])
    wrg = dt_in("wrg", [16, 128, 1024])
    wpool = dt_in("wpool", [8, 128, 4096])
    wbl = dt_in("wbl", [32, 128, 4096])
    wbp = dt_in("wbp", [32, 128, 4096])
    wo = dt_in("wo", [32, 128, 4096])
    wgu = dt_in("wgu", [172, 128, 4096])
    wd = dt_in("wd", [32, 128, DFF])
    wada = dt_in("wada", [192, 128, 4096])
    y = nc.dram_tensor("y", [TOK_CORE, D], F32, kind="ExternalOutput").ap()
    ysc_d = nc.dram_tensor("ysc_d", [T, D], F32).ap()
    x1_d = nc.dram_tensor("x1_d", [T, D], F32).ap()

    with contextlib.ExitStack() as es:
        P = Prog(nc, es)
        sbt = lambda name, shape, dt: es.enter_context(nc.sbuf_tensor(name, shape, dt))
        u = sbt("u", [128, 32, T], BF16)
        bcd = sbt("bcd", [128, 96 * 256], F32)
        scr = sbt("scr", [128, 9248], F32)
        wslot = [sbt(f"wslot{i}", [128, 4096], BF16) for i in range(NW)]
        vecT = sbt("vecT", [128, 640], F32)
        modT = sbt("modT", [128, 192], F32)
        der = sbt("der", [128, 6 * 32], F32)
        nsp = sbt("nsp", [128, 64], F32)
        state = sbt("state", [128, 32], F32)
        xr_tail = sbt("xr_tail", [128, 32 * 3], F32)
        xp_tail = sbt("xp_tail", [128, 32 * 15], F32)
        identf = sbt("identf", [128, 128], F32)
        identb = sbt("identb", [128, 128], BF16)
        onesc = sbt("onesc", [128, 1], F32)
        flg = sbt("flg", [128, 16], F32)
        invt = sbt("invt", [128, 64], F32)
        cact = sbt("cact", [128, 32], BF16)
        small = sbt("small", [128, 32], F32)
        psum = [es.enter_context(nc.psum_tensor(f"ps{i}", [128, 512], F32)) for i in range(8)]
        block = es.enter_context(nc.Block())

        b_u = [Buf(f"u{c}") for c in range(32)]
        b_bcd = [Buf(f"bcd{c}") for c in range(96)]
        b_w = [Buf(f"w{i}") for i in range(NW)]
        b_ps = [Buf(f"ps{i}") for i in range(8)]
        b_vecT, b_modT, b_der, b_nsp = Buf(), Buf(), Buf(), Buf()
        b_state = [Buf() for _ in range(32)]
        b_xrt = [Buf() for _ in range(32)]
        b_xpt = [Buf() for _ in range(32)]
        b_const = Buf()
        b_small = Buf()
        b_small2 = Buf()
        b_ysc_d = [Buf() for _ in range(4)]
        b_x1_d = [Buf() for _ in range(4)]

        def bview(c):
            return bcd[:, c * 256:(c + 1) * 256].bitcast(BF16)

        xblk = [bcd[:, i * 4096:(i + 1) * 4096] for i in range(2)]
        yscp = [bcd[:, 8192 + i * 512: 8192 + (i + 1) * 512] for i in range(2)]
        xnp_ = [bcd[:, 9216 + i * 512: 9216 + (i + 1) * 512].bitcast(BF16) for i in range(2)]
        sqj = bcd[:, 10240:10752].bitcast(BF16)
        b_xblk = [Buf(), Buf()]
        b_yscp = [Buf(), Buf()]
        b_xnp = [Buf(), Buf()]
        b_sqj = Buf()

        ring = [0]
        accn = [0]
        wep = [0]

        def wload(src, ncols=4096):
            s = ring[0] % NW
            ring[0] += 1
            P.dma("pool", f"w{s}_{wep[0]}", lambda e: e.dma_start(out=wslot[s][:, 0:ncols], in_=src), writes=[b_w[s]])
            return s

        def acc():
            b = accn[0] % 4
            accn[0] += 1
            return b

        def mm_group(out_ap, out_buf, pairs, reads):
            n = len(pairs)
            for i, (l, r) in enumerate(pairs):
                P.op("pe", lambda e, l=l, r=r, i=i: e.matmul(out_ap, lhsT=l, rhs=r, start=(i == 0), stop=(i == n - 1)),
                     reads=reads, writes=[out_buf], inc=(i == n - 1))

        def act(out, in_, func, reads, writes, bias=None, scale=None, accum=None):
            kw = {}
            if bias is not None:
                kw["bias"] = bias
            if scale is not None:
                kw["scale"] = scale
            if accum is not None:
                kw["accum_out"] = accum
            return P.op("act", lambda e: e.activation(out=out, in_=in_, func=func, **kw), reads=reads, writes=writes)

        def tt(out, in0, in1, op, reads, writes, eng="dve"):
            return P.op(eng, lambda e: e.tensor_tensor(out=out, in0=in0, in1=in1, op=op), reads=reads, writes=writes)

        def ts(out, in0, s1, s2, op0, op1, reads, writes):
            if s2 is None:
                return P.op("dve", lambda e: e.tensor_scalar(out=out, in0=in0, scalar1=s1, scalar2=None, op0=op0), reads=reads, writes=writes)
            return P.op("dve", lambda e: e.tensor_scalar(out=out, in0=in0, scalar1=s1, scalar2=s2, op0=op0, op1=op1), reads=reads, writes=writes)

        def stt(out, in0, scalar, in1, op0, op1, reads, writes):
            return P.op("dve", lambda e: e.scalar_tensor_tensor(out=out, in0=in0, scalar=scalar, in1=in1, op0=op0, op1=op1), reads=reads, writes=writes)

        def cp(out, in_, reads, writes):
            return P.op("dve", lambda e: e.tensor_copy(out=out, in_=in_), reads=reads, writes=writes)

        def V(i):
            return vecT[:, i * 32:(i + 1) * 32]

        def Vc(i, c):
            return vecT[:, i * 32 + c: i * 32 + c + 1]

        P.dma("sp", "c0", lambda e: e.dma_start(out=identf[:], in_=ident_d), writes=[b_const])
        P.dma("sp", "c1", lambda e: e.dma_start(out=flg[:], in_=flags), writes=[b_const])
        P.dma("sp", "c2", lambda e: e.dma_start(out=invt[:], in_=invc), writes=[b_const])
        cp(identb[:], identf[:], [b_const], [b_const])
        P.op("dve", lambda e: e.memset(onesc[:], 1.0), writes=[b_const])
        P.op("dve", lambda e: e.memset(state[:], 0.0), writes=b_state)
        P.op("dve", lambda e: e.memset(xr_tail[:], 0.0), writes=b_xrt)
        P.op("dve", lambda e: e.memset(xp_tail[:], 0.0), writes=b_xpt)
        b_vst = [Buf() for _ in range(5)]
        for i in range(5):
            P.dma("sp", f"v{i}", lambda e, i=i: e.dma_start(out=scr[:, i * 128:(i + 1) * 128], in_=vecs[i * 128:(i + 1) * 128, :]), writes=[b_vst[i]])
        for i in range(5):
            pb = 4 + (i % 2)
            P.op("pe", lambda e, i=i, pb=pb: e.transpose(out=psum[pb][:, 0:128], in_=scr[:, i * 128:(i + 1) * 128], identity=identf[:]),
                 reads=[b_vst[i], b_const], writes=[b_ps[pb]])
            cp(vecT[:, i * 128:(i + 1) * 128], psum[pb][:, 0:128], [b_ps[pb]], [b_vecT])
        act(cact[:], V(V_C), AF.Silu, [b_vecT], [b_const])
        act(nsp[:, 0:32], V(V_LAM), AF.Exp, [b_vecT], [b_nsp], scale=-1.0)
        act(nsp[:, 0:32], nsp[:, 0:32], AF.Ln, [b_nsp], [b_nsp], bias=1.0)
        ts(nsp[:, 32:64], nsp[:, 0:32], -16.0, None, ALU.mult, None, [b_nsp], [b_nsp])
        ts(nsp[:, 0:32], nsp[:, 0:32], -8.0, None, ALU.mult, None, [b_nsp], [b_nsp])

        b_row = [Buf(), Buf()]
        rowsb = [scr[0:1, 1024 + i * 512: 1024 + (i + 1) * 512] for i in range(2)]
        rowx = sbt("rowx", [1, 512], F32)
        b_rowx = Buf()
        ada_next = [0]

        def adaln_groups(ng, in_prefix):
            for g in range(ada_next[0], min(48, ada_next[0] + ng)):
                rb = 7 if in_prefix else (7, 5)[g % 2]
                rsb, rbuf = (rowx[0:1, :], b_rowx) if in_prefix else (rowsb[g % 2], b_row[g % 2])
                for q in range(4):
                    s = wload(wada[g * 4 + q])
                    for kk in range(8):
                        kc = q * 8 + kk
                        P.op("pe", lambda e, s=s, kk=kk, kc=kc, rb=rb: e.matmul(psum[rb][0:1, :], lhsT=cact[:, kc:kc + 1], rhs=wslot[s][:, kk * 512:(kk + 1) * 512],
                                                                              start=(kc == 0), stop=(kc == 31)),
                             reads=[b_w[s], b_const], writes=[b_ps[rb]], inc=(kc == 31))
                act(rsb, psum[rb][0:1, :], AF.Copy, [b_ps[rb]], [rbuf])
                for q in range(4):
                    P.op("pe", lambda e, g=g, q=q, rsb=rsb: e.matmul(psum[6][:, g * 4 + q: g * 4 + q + 1], lhsT=rsb[0:1, q * 128:(q + 1) * 128], rhs=identf[0:1, 0:1],
                                                                    start=True, stop=True),
                         reads=[rbuf, b_const], writes=[b_ps[6]])
                ada_next[0] = g + 1

        adaln_groups(16, False)
        tt(modT[:, 0:64], psum[6][:, 0:64], vecT[:, 448:512], ALU.add, [b_ps[6], b_vecT], [b_modT])
        GM_M, SH_M, GTG_M, GM_F, SH_F, GTG_F = range(6)

        def DER(i, c):
            return der[:, i * 32 + c: i * 32 + c + 1]

        stt(der[:, 0:32], modT[:, 32:64], 1.0, V(V_GMP), ALU.add, ALU.mult, [b_modT, b_vecT], [b_der])
        cp(der[:, 32:64], modT[:, 0:32], [b_modT], [b_der])

        def adaln_finish():
            adaln_groups(48, False)
            tt(modT[:, 64:192], psum[6][:, 64:192], vecT[:, 512:640], ALU.add, [b_ps[6], b_vecT], [b_modT])
            tt(der[:, 64:96], modT[:, 64:96], V(V_GMPOST), ALU.mult, [b_modT, b_vecT], [b_der])
            stt(der[:, 96:128], modT[:, 128:160], 1.0, V(V_GFP), ALU.add, ALU.mult, [b_modT, b_vecT], [b_der])
            cp(der[:, 128:160], modT[:, 96:128], [b_modT], [b_der])
            tt(der[:, 160:192], modT[:, 160:192], V(V_GFPOST), ALU.mult, [b_modT, b_vecT], [b_der])

        def prologue_block(i, tb, gm_i, sh_i):
            xb = xblk[i]
            P.op("dve", lambda e: e.memset(small[:, 0:4], 0.0), writes=[b_small])
            for q in range(4):
                act(sqj, xb[:, q * 1024:(q + 1) * 1024], AF.Square, [b_xblk[i], b_small], [b_sqj, b_small], accum=small[:, q:q + 1])
            P.op("dve", lambda e: e.reduce_sum(out=small[:, 4:5], in_=small[:, 0:4], axis=mybir.AxisListType.X), reads=[b_small], writes=[b_small])
            act(small[:, 5:6], small[:, 4:5], AF.Sqrt, [b_small], [b_small], bias=EPS, scale=1.0 / D)
            P.op("dve", lambda e: e.reciprocal(out=small[:, 6:7], in_=small[:, 5:6]), reads=[b_small], writes=[b_small])
            for q in range(4):
                xi = q % 2
                act(xnp_[xi], xb[:, q * 1024:(q + 1) * 1024], AF.Copy, [b_xblk[i], b_small], [b_xnp[xi]], scale=small[:, 6:7])
                pb = 4 + xi
                pst = psum[pb][:, 0:512].bitcast(BF16)
                for j in range(8):
                    P.op("pe", lambda e, j=j, xi=xi, pst=pst: e.transpose(out=pst[:, j * 128:(j + 1) * 128], in_=xnp_[xi][:, j * 128:(j + 1) * 128], identity=identb[:]),
                         reads=[b_xnp[xi], b_const], writes=[b_ps[pb]])
                for j in range(8):
                    c = q * 8 + j
                    o = u[:, c, tb * 128:(tb + 1) * 128]
                    src = pst[:, j * 128:(j + 1) * 128]
                    if pb == 4:
                        ts(o, src, DER(gm_i, c), DER(sh_i, c), ALU.mult, ALU.add, [b_ps[pb], b_der], [b_u[c]])
                    else:
                        act(o, src, AF.Identity, [b_ps[pb], b_der], [b_u[c]], bias=DER(sh_i, c), scale=DER(gm_i, c))

        def prologue_from_dram(rows_ap_fn, gm_i, sh_i):
            for tb in range(4):
                i = tb % 2
                P.dma("sp", f"xb{i}", lambda e, i=i, tb=tb: e.dma_start(out=xblk[i], in_=rows_ap_fn(tb)), writes=[b_xblk[i]])
                prologue_block(i, tb, gm_i, sh_i)

        XR = lambda k: scr[:, k * 520: k * 520 + 515]
        XC = lambda k: scr[:, 2080 + k * 512: 2080 + (k + 1) * 512]
        XCB = lambda k: scr[:, 4128 + k * 256: 4128 + (k + 1) * 256].bitcast(BF16)
        TMP = lambda s_, k: scr[:, 5152 + (s_ * 4 + k) * 512: 5152 + (s_ * 4 + k + 1) * 512]
        b_xr = [Buf() for _ in range(4)]
        b_xc = [Buf() for _ in range(4)]
        b_xcb = [Buf() for _ in range(4)]
        b_tmp = [[Buf() for _ in range(4)] for _ in range(2)]

        def lru_stage_a(hd, flag_col):
            for jj in range(2):
                j = 2 * hd + jj
                k = (hd % 2) * 2 + jj
                s = wload(win[j])
                b = acc()
                mm_group(psum[b][:, :], b_ps[b], [(wslot[s][:, kc * 128:(kc + 1) * 128], u[:, kc, :]) for kc in range(32)], [b_w[s]] + b_u)
                xr = XR(k)
                act(xr[:, 3:515], psum[b][:, :], AF.Copy, [b_ps[b]], [b_xr[k]])
                cp(xr[:, 0:3], xr_tail[:, j * 3:(j + 1) * 3], [b_xrt[j]], [b_xr[k]])
                xc = XC(k)
                ts(xc, xr[:, 3:515], Vc(V_CW3, j), Vc(V_CB, j), ALU.mult, ALU.add, [b_xr[k], b_vecT], [b_xc[k]])
                for kk in (2, 1, 0):
                    stt(xc, xr[:, kk:kk + 512], Vc(V_CW0 + kk, j), xc, ALU.mult, ALU.add, [b_xr[k], b_vecT, b_xc[k]], [b_xc[k]])
                if flag_col is None:
                    cp(xr_tail[:, j * 3:(j + 1) * 3], xr[:, 512:515], [b_xr[k]], [b_xrt[j]])
                else:
                    ts(xr_tail[:, j * 3:(j + 1) * 3], xr[:, 512:515], flg[:, flag_col:flag_col + 1], None, ALU.mult, None, [b_xr[k], b_const], [b_xrt[j]])
                act(XCB(k), xc, AF.Copy, [b_xc[k]], [b_xcb[k]])

        def lru_stage_b(hd, flag_col, main):
            s = wload(wrg[hd], 1024)
            kbase = (hd % 2) * 2
            sg = None
            if main:
                sg = [wload(win[32 + 2 * hd + jo]) for jo in range(2)]
            for jo in range(2):
                j = 2 * hd + jo
                R, I, A_, M = (TMP(jo, 0), TMP(jo, 1), TMP(jo, 2), TMP(jo, 3))
                bR, bI, bA, bM = b_tmp[jo]
                for gate in range(2):
                    b = acc()
                    mm_group(psum[b][:, :], b_ps[b],
                             [(wslot[s][:, ((gate * 2 + ki) * 2 + jo) * 128:((gate * 2 + ki) * 2 + jo + 1) * 128], XCB(kbase + ki)) for ki in range(2)],
                             [b_w[s], b_xcb[kbase], b_xcb[kbase + 1]])
                    if gate == 0:
                        act(R, psum[b][:, :], AF.Sigmoid, [b_ps[b], b_vecT], [bR], bias=Vc(V_BA, j))
                    else:
                        act(I, psum[b][:, :], AF.Sigmoid, [b_ps[b], b_vecT], [bI], bias=Vc(V_BX, j))
                act(A_, R, AF.Exp, [bR, b_nsp], [bA], scale=nsp[:, j:j + 1])
                act(M, R, AF.Exp, [bR, b_nsp], [bM], scale=nsp[:, 32 + j:33 + j])
                act(M, M, AF.Sqrt, [bM], [bM], bias=1.0, scale=-1.0)
                tt(I, I, XC(kbase + jo), ALU.mult, [bI, b_xc[kbase + jo]], [bI])
                tt(I, I, M, ALU.mult, [bI, bM], [bI])
                P.op("dve", lambda e, M=M, A_=A_, I=I, j=j: e.tensor_tensor_scan(out=M, data0=A_, data1=I, initial=state[:, j:j + 1], op0=ALU.mult, op1=ALU.add),
                     reads=[bA, bI, b_state[j], bM], writes=[bM])
                if flag_col is None:
                    cp(state[:, j:j + 1], M[:, 511:512], [bM], [b_state[j]])
                else:
                    ts(state[:, j:j + 1], M[:, 511:512], flg[:, flag_col:flag_col + 1], None, ALU.mult, None, [bM, b_const], [b_state[j]])
                if main:
                    b = acc()
                    s2 = sg[jo]
                    mm_group(psum[b][:, :], b_ps[b], [(wslot[s2][:, kc * 128:(kc + 1) * 128], u[:, kc, :]) for kc in range(32)], [b_w[s2]] + b_u)
                    act(R, psum[b][:, :], AF.Gelu_apprx_tanh, [b_ps[b], bR], [bR])
                    tt(bview(j), M, R, ALU.mult, [bM, bR], [b_bcd[j]])

        def lru_branch(flag_col, main):
            lru_stage_a(0, flag_col)
            for hd in range(16):
                if hd + 1 < 16:
                    lru_stage_a(hd + 1, flag_col)
                lru_stage_b(hd, flag_col, main)

        XP = lambda k: scr[:, k * 528: k * 528 + 527]
        TA = lambda k, w_: scr[:, 1056 + (2 * k + w_) * 528: 1056 + (2 * k + w_) * 528 + 527]
        PO = lambda k: scr[:, 3168 + k * 256: 3168 + (k + 1) * 256].bitcast(BF16)
        T16 = lambda k: scr[:, 5216 + k * 16: 5216 + (k + 1) * 16]
        b_xp = [Buf(), Buf()]
        b_ta = [[Buf(), Buf()], [Buf(), Buf()]]
        b_po = [Buf() for _ in range(8)]
        b_t16 = [Buf(), Buf()]

        def pool_branch(first_tile):
            cnt = 0
            for g in range(4):
                w_ = POOL_W[g]
                for k in range(8):
                    c = 8 * g + k
                    i = cnt % 2
                    cnt += 1
                    s = wload(win[64 + c])
                    b = acc()
                    mm_group(psum[b][:, :], b_ps[b], [(wslot[s][:, kc * 128:(kc + 1) * 128], u[:, kc, :]) for kc in range(32)], [b_w[s]] + b_u)
                    xp = XP(i)
                    act(xp[:, 15:527], psum[b][:, :], AF.Copy, [b_ps[b]], [b_xp[i]])
                    cp(xp[:, 0:15], xp_tail[:, c * 15:(c + 1) * 15], [b_xpt[c]], [b_xp[i]])
                    cur, cur_b, lo, step, wi = xp, b_xp[i], 1, 1, 0
                    for lvl in range(g + 1):
                        dst, dst_b = TA(i, wi), b_ta[i][wi]
                        tt(dst[:, lo:527], cur[:, lo:527], cur[:, lo - step:527 - step], ALU.add, [cur_b], [dst_b])
                        cur, cur_b = dst, dst_b
                        step *= 2
                        lo = 2 * step - 1
                        wi ^= 1
                    stt(PO(k), cur[:, 15:527], 1.0 / w_, xp[:, 15:527], ALU.mult, ALU.subtract, [cur_b, b_xp[i]], [b_po[k]])
                    if first_tile:
                        tt(T16(i), cur[:, 15:31], invt[:, g * 16:(g + 1) * 16], ALU.mult, [cur_b, b_const], [b_t16[i]])
                        tt(PO(k)[:, 0:16], T16(i), xp[:, 15:31], ALU.subtract, [b_t16[i], b_xp[i]], [b_po[k]])
                    act(xp_tail[:, c * 15:(c + 1) * 15], xp[:, 512:527], AF.Copy, [b_xp[i]], [b_xpt[c]])
                for nh in range(2):
                    s = wload(wpool[g * 2 + nh])
                    for n4 in range(4):
                        n = 8 * g + nh * 4 + n4
                        b = acc()
                        mm_group(psum[b][:, :], b_ps[b],
                                 [(wslot[s][:, kc * 512 + n4 * 128: kc * 512 + (n4 + 1) * 128], PO(kc)) for kc in range(8)],
                                 [b_w[s]] + b_po)
                        act(bview(32 + n), psum[b][:, :], AF.Copy, [b_ps[b], b_vecT], [b_bcd[32 + n]], scale=Vc(V_PS, n))

        S1 = lambda i, k: scr[:, (k * 2 + i) * 512: (k * 2 + i + 1) * 512]
        b_m3 = [[Buf() for _ in range(4)] for _ in range(2)]

        def merge_phase():
            rhs_u = [u[:, kc, :] for kc in range(32)]
            for n in range(32):
                i = n % 2
                s1, t1, s3, t2 = (S1(i, k) for k in range(4))
                bs1, bt1, bs3, bt2 = b_m3[i]
                for br in range(2):
                    sw = wload(win[96 + 32 * br + n])
                    b = acc()
                    mm_group(psum[b][:, :], b_ps[b], [(wslot[sw][:, kc * 128:(kc + 1) * 128], rhs_u[kc]) for kc in range(32)], [b_w[sw]] + b_u)
                    act(s1 if br == 0 else s3, psum[b][:, :], AF.Sigmoid, [b_ps[b]], [bs1 if br == 0 else bs3])
                    sw = wload((wbl if br == 0 else wbp)[n])
                    b = acc()
                    yb = b_bcd[32 * br:32 * br + 32]
                    mm_group(psum[b][:, :], b_ps[b], [(wslot[sw][:, kc * 128:(kc + 1) * 128], bview(32 * br + kc)) for kc in range(32)], [b_w[sw]] + yb)
                    if br == 0:
                        tt(t1, s1, psum[b][:, :], ALU.mult, [bs1, b_ps[b]], [bt1])
                    else:
                        tt(t2, s3, psum[b][:, :], ALU.mult, [bs3, b_ps[b]], [bt2])
                tt(bview(64 + n), t1, t2, ALU.add, [bt1, bt2], [b_bcd[64 + n]])

        YSC = lambda i: scr[:, 4096 + i * 512: 4096 + (i + 1) * 512]
        SQ = lambda i: scr[:, 5120 + i * 512: 5120 + (i + 1) * 512]
        YTK = lambda i: scr[:, 6144 + i * 512: 6144 + (i + 1) * 512]
        b_ysc = [Buf(), Buf()]
        b_sq = [Buf(), Buf()]
        b_ytk = [Buf(), Buf()]
        ysc_d_v = ysc_d.rearrange("(tb p) f -> p tb f", p=128)
        ysc_toks = [None, None]

        def epi_chunk(n, b, gtg_i):
            i = n % 2
            act(YSC(i), psum[b][:, :], AF.Copy, [b_ps[b], b_der], [b_ysc[i]], scale=DER(gtg_i, n))
            act(SQ(i), psum[b][:, :], AF.Square, [b_ps[b]], [b_sq[i]])
            P.op("pe", lambda e, i=i, n=n: e.matmul(psum[6][0:1, :], lhsT=onesc[:, 0:1], rhs=SQ(i), start=(n == 0), stop=(n == 31)),
                 reads=[b_sq[i], b_const], writes=[b_ps[6]])
            pb = 4 + i
            for tb in range(4):
                P.op("pe", lambda e, i=i, tb=tb, pb=pb: e.transpose(out=psum[pb][:, tb * 128:(tb + 1) * 128], in_=YSC(i)[:, tb * 128:(tb + 1) * 128], identity=identf[:]),
                     reads=[b_ysc[i], b_const], writes=[b_ps[pb]])
            cp(YTK(i), psum[pb][:, :], [b_ps[pb]], [b_ytk[i]])
            ysc_toks[i] = P.dma("sp", f"yk{i}", lambda e, i=i, n=n: e.dma_start(out=ysc_d_v[:, :, n * 128:(n + 1) * 128], in_=YTK(i).rearrange("p (tb f) -> p tb f", tb=4)),
                  reads=[b_ytk[i]])

        def epi_rstd():
            row = scr[0:1, 7168:7680]
            b_r = Buf()
            act(row, psum[6][0:1, :], AF.Sqrt, [b_ps[6]], [b_r], bias=EPS, scale=1.0 / D)
            P.op("dve", lambda e: e.reciprocal(out=row, in_=row), reads=[b_r], writes=[b_r])
            for tb in range(4):
                P.op("pe", lambda e, tb=tb: e.matmul(psum[7][:, tb:tb + 1], lhsT=row[0:1, tb * 128:(tb + 1) * 128], rhs=identf[0:1, 0:1], start=True, stop=True),
                     reads=[b_r, b_const], writes=[b_ps[7]])
            cp(small[:, 12:16], psum[7][:, 0:4], [b_ps[7]], [b_small2])

        def pass2_block(tb, i, xsrc_ap, xsrc_bufs):
            P.dma("sp", f"xb{i}", lambda e: e.dma_start(out=xblk[i], in_=xsrc_ap), reads=xsrc_bufs, writes=[b_xblk[i]])
            for q in range(8):
                yi = q % 2
                P.dma("sp", f"yp{yi}", lambda e, yi=yi, q=q: e.dma_start(out=yscp[yi], in_=ysc_d[tb * 128:(tb + 1) * 128, q * 512:(q + 1) * 512]),
                      writes=[b_yscp[yi]], extra=list(ysc_toks))
                stt(xblk[i][:, q * 512:(q + 1) * 512], yscp[yi], small[:, 12 + tb:13 + tb], xblk[i][:, q * 512:(q + 1) * 512], ALU.mult, ALU.add,
                    [b_yscp[yi], b_small2, b_xblk[i]], [b_xblk[i]])

        for pt in range(NT_PRE - n_pre, NT_PRE):
            P.fence()
            prologue_from_dram(lambda tb, pt=pt: xprev[pt * T + tb * 128: pt * T + (tb + 1) * 128, :], GM_M, SH_M)
            P.fence()
            lru_branch(pt, False)
            if pt < NT_PRE - 1:
                adaln_groups(4, True)
            if pt == NT_PRE - 1:
                for c in range(32):
                    s = wload(win[64 + c])
                    hb = (7, 5)[c % 2]
                    hr = psum[hb][:, 0:16]
                    mm_group(hr, b_ps[hb], [(wslot[s][:, kc * 128:(kc + 1) * 128], u[:, kc, 496:512]) for kc in range(32)], [b_w[s]] + b_u)
                    ts(xp_tail[:, c * 15:(c + 1) * 15], hr[:, 1:16], flg[:, pt:pt + 1], None, ALU.mult, None, [b_ps[hb], b_const], [b_xpt[c]])

        P.fence()
        adaln_finish()
        out_toks = []
        for t in range(n_main):
            P.fence()
            if t % 4 == 0:
                P.new_epoch()
                wep[0] = 1 + t // 4
            prologue_from_dram(lambda tb, t=t: xm[t * T + tb * 128: t * T + (tb + 1) * 128, :], GM_M, SH_M)
            P.fence()
            lru_branch(None, True)
            P.fence()
            pool_branch(t == 0)
            P.fence()
            merge_phase()
            for n in range(32):
                s = wload(wo[n])
                b = acc()
                mm_group(psum[b][:, :], b_ps[b], [(wslot[s][:, kc * 128:(kc + 1) * 128], bview(64 + kc)) for kc in range(32)], [b_w[s]] + b_bcd[64:96])
                epi_chunk(n, b, GTG_M)
            epi_rstd()
            P.fence()
            for tb in range(4):
                i = tb % 2
                pass2_block(tb, i, xm[t * T + tb * 128: t * T + (tb + 1) * 128, :], [])
                P.dma("sp", f"xs{i}", lambda e, i=i, tb=tb: e.dma_start(out=x1_d[tb * 128:(tb + 1) * 128, :], in_=xblk[i]), reads=[b_xblk[i]], writes=[b_x1_d[tb]])
                prologue_block(i, tb, GM_F, SH_F)
            P.fence()
            b_sg = [Buf(), Buf()]
            rhs_u = [u[:, kc, :] for kc in range(32)]
            for j in range(HC):
                i = j % 2
                sG = wload(wgu[j])
                bG = acc()
                mm_group(psum[bG][:, :], b_ps[bG], [(wslot[sG][:, kc * 128:(kc + 1) * 128], rhs_u[kc]) for kc in range(32)], [b_w[sG]] + b_u)
                sgv = scr[:, i * 512:(i + 1) * 512]
                act(sgv, psum[bG][:, :], AF.Silu, [b_ps[bG]], [b_sg[i]])
                sU = wload(wgu[HC + j])
                bU = acc()
                mm_group(psum[bU][:, :], b_ps[bU], [(wslot[sU][:, kc * 128:(kc + 1) * 128], rhs_u[kc]) for kc in range(32)], [b_w[sU]] + b_u)
                tt(bview(j), sgv, psum[bU][:, :], ALU.mult, [b_sg[i], b_ps[bU]], [b_bcd[j]])
            for n in range(32):
                ss = [wload(wd[n][:, 0:4096]), wload(wd[n][:, 4096:8192]), wload(wd[n][:, 8192:DFF], DFF - 8192)]
                b = acc()
                mm_group(psum[b][:, :], b_ps[b],
                         [(wslot[ss[kc // 32]][:, (kc % 32) * 128:(kc % 32 + 1) * 128], bview(kc)) for kc in range(HC)],
                         [b_w[s_] for s_ in ss] + b_bcd[0:HC])
                epi_chunk(n, b, GTG_F)
            epi_rstd()
            P.fence()
            for tb in range(4):
                i = tb % 2
                pass2_block(tb, i, x1_d[tb * 128:(tb + 1) * 128, :], [b_x1_d[tb]])
                out_toks.append(P.dma("sp", f"xs{i}", lambda e, i=i, tb=tb, t=t: e.dma_start(out=y[t * T + tb * 128: t * T + (tb + 1) * 128, :], in_=xblk[i]),
                                      reads=[b_xblk[i]]))
        P.fence()
        P.wait_all("sp", out_toks)
        P.emit(block)
    return nc


def _tile_cols(w):
    K, N = w.shape
    return np.ascontiguousarray(w.reshape(K // 128, 128, N // 128, 128).transpose(2, 1, 0, 3)).reshape(N // 128, 128, K)


def _prep_shared(inp):
    sh = {}
    sh["win"] = _tile_cols(inp["w_in"][0])
    wa = inp["w_rg_a"][0].reshape(16, 2, 128, 2, 128).transpose(0, 2, 1, 3, 4)
    wx = inp["w_rg_x"][0].reshape(16, 2, 128, 2, 128).transpose(0, 2, 1, 3, 4)
    sh["wrg"] = np.ascontiguousarray(np.stack([wa, wx], axis=2)).reshape(16, 128, 1024)
    sh["wpool"] = np.ascontiguousarray(inp["pool_w"][0].reshape(4, 8, 128, 2, 512).transpose(0, 3, 2, 1, 4)).reshape(8, 128, 4096)
    sh["wbl"] = _tile_cols(inp["w_branch_lru"][0])
    sh["wbp"] = _tile_cols(inp["w_branch_pool"][0])
    sh["wo"] = _tile_cols(inp["w_o"][0])
    sh["wgu"] = _tile_cols(inp["w_gate_up"][0])
    sh["wd"] = _tile_cols(inp["w_down"][0])
    sh["wada"] = np.ascontiguousarray(inp["w_ada"][0].reshape(4, 8, 128, 48, 512).transpose(3, 0, 2, 1, 4)).reshape(192, 128, 4096)
    sh["ident"] = np.eye(128, dtype=np.float32)
    return sh


def _vec_rows(inp, b):
    rows = [inp["g_mix_pre"][0], inp["g_mix_post"][0], inp["g_ffn_pre"][0], inp["g_ffn_post"][0],
            inp["conv_w"][0, 0], inp["conv_w"][0, 1], inp["conv_w"][0, 2], inp["conv_w"][0, 3],
            inp["conv_b"][0], inp["b_rg_a"][0], inp["b_rg_x"][0], inp["lru_lambda"][0], inp["pool_scale"][0],
            inp["c"][b]]
    v = np.concatenate([np.asarray(r, np.float32).reshape(32, 128) for r in rows] + [np.asarray(inp["b_ada"][0], np.float32).reshape(192, 128)], axis=0)
    return np.ascontiguousarray(v)


def _core_inputs(inp, sh, c):
    b, j = divmod(c, 4)
    x = inp["x"]
    m = dict(sh)
    m["xm"] = np.ascontiguousarray(x[b, j * TOK_CORE:(j + 1) * TOK_CORE, :])
    xp = np.zeros((NT_PRE * T, D), np.float32)
    start = j * TOK_CORE - NT_PRE * T
    if j > 0:
        xp[-j * TOK_CORE:, :] = x[b, 0:j * TOK_CORE, :]
    m["xprev"] = xp
    fl = np.zeros((128, 16), np.float32)
    for pt in range(NT_PRE):
        if start + pt * T >= 0:
            fl[:, pt] = 1.0
    m["flags"] = fl
    iv = np.zeros((128, 64), np.float32)
    for g, w_ in enumerate(POOL_W):
        for t in range(16):
            iv[:, g * 16 + t] = 1.0 / (min(t + 1, w_) if j == 0 else w_)
    m["invc"] = iv
    m["vecs"] = _vec_rows(inp, b)
    return m


_NC_CACHE = {}


def kernel(**inputs):
    inp = {k: np.asarray(v, dtype=np.float32) for k, v in inputs.items()}
    sh = _prep_shared(inp)
    in_maps = [_core_inputs(inp, sh, c) for c in range(N_CORES)]
    if "nc" not in _NC_CACHE:
        _NC_CACHE["nc"] = build_nc()
    nc = _NC_CACHE["nc"]
    res = run_bass_kernel_spmd(nc, in_maps, core_ids=list(range(N_CORES)))
    out = np.empty((2, 8192, D), np.float32)
    for c in range(N_CORES):
        b, j = divmod(c, 4)
        out[b, j * TOK_CORE:(j + 1) * TOK_CORE, :] = res.results[c]["y"]
    return out
```

```python
import contextlib
import numpy as np
import concourse.bass as bass
import concourse.mybir as mybir
from concourse.bass_utils import run_bass_kernel_spmd

F32 = mybir.dt.float32
BF16 = mybir.dt.bfloat16
AF = mybir.ActivationFunctionType
ALU = mybir.AluOpType

D = 4096
T = 512
NT_MAIN = 4
NT_PRE = 12
DFF = 11008
HC = DFF // 128
NW = 4
EPS = 1e-6
N_CORES = 8
TOK_CORE = 2048

ENGS = ("pe", "act", "dve", "pool", "sp")


class Buf:
    __slots__ = ("name", "w", "r")

    def __init__(self, name=""):
        self.name = name
        self.w = None
        self.r = set()


class Prog:
    def __init__(self, nc, es):
        self.nc = nc
        self.es = es
        self.q = {e: [] for e in ENGS}
        self.epoch = {e: 0 for e in ENGS}
        self.sem = {("tl", e, 0): es.enter_context(nc.semaphore("tl_" + e + "_0")) for e in ENGS}
        self.cnt = {e: 0 for e in ENGS}
        self.pending = {e: False for e in ENGS}
        self.seen = {e: {} for e in ENGS}
        self.dsem = {}
        self.dcnt = {}
        self.sp_out = {}

    def _deps(self, eng, reads, writes, extra):
        best = {}
        seen = self.seen[eng]

        def add(t):
            k, v = t
            if eng == "pe" and k[0] == "tl" and k[1] == "pe":
                return
            if seen.get(k, 0) >= v:
                return
            if best.get(k, 0) < v:
                best[k] = v

        for b in reads:
            if b.w is not None:
                add(b.w)
        for b in writes:
            if b.w is not None:
                add(b.w)
            for t in b.r:
                add(t)
        for t in extra:
            if t is not None:
                add(t)
        for k, v in best.items():
            seen[k] = v
        return list(best.items())

    def _semobj(self, k):
        return self.sem[k] if k[0] == "tl" else self.dsem[k[1]]

    def tlkey(self, e):
        return ("tl", e, self.epoch[e])

    def new_epoch(self):
        for e in ("pe", "act", "dve"):
            assert not self.pending[e]
            self.epoch[e] += 1
            self.sem[self.tlkey(e)] = self.es.enter_context(self.nc.semaphore(f"tl_{e}_{self.epoch[e]}"))
            self.cnt[e] = 0

    def _mark(self, tok, reads, writes):
        for b in reads:
            b.r.add(tok)
        for b in writes:
            b.w = tok
            b.r = set()

    def op(self, eng, fn, reads=(), writes=(), inc=True, extra=()):
        waits = self._deps(eng, reads, writes, extra)
        if inc:
            self.cnt[eng] += 1
            self.pending[eng] = False
            tok = (self.tlkey(eng), self.cnt[eng])
        else:
            self.pending[eng] = True
            tok = (self.tlkey(eng), self.cnt[eng] + 1)
        self.q[eng].append((waits, fn, self.tlkey(eng) if inc else None, 1))
        self._mark(tok, reads, writes)
        return tok

    def dma(self, eng, key, fn, reads=(), writes=(), extra=()):
        if key not in self.dsem:
            self.dsem[key] = self.es.enter_context(self.nc.semaphore("d_" + str(key)))
            self.dcnt[key] = 0
        waits = self._deps(eng, reads, writes, extra)
        self.dcnt[key] += 16
        tok = (("d", key), self.dcnt[key])
        self.q[eng].append((waits, fn, ("d", key), 16))
        self._mark(tok, reads, writes)
        if eng == "sp":
            self.sp_out[key] = tok
        return tok

    def fence(self):
        for e in ("pe", "act", "dve"):
            assert not self.pending[e]
        toks = [(self.tlkey(e), self.cnt[e]) for e in ("pe", "act", "dve") if self.cnt[e] > 0]
        toks += list(self.sp_out.values())
        for e in ("pe", "act", "dve", "sp"):
            waits = self._deps(e, (), (), toks)
            if waits:
                self.q[e].append((waits, None, None, 0))

    def wait_all(self, eng, toks):
        waits = self._deps(eng, (), (), toks)
        self.q[eng].append((waits, None, None, 0))

    def emit(self, block):
        for e in ENGS:
            assert not self.pending[e], e

        def run(engobj, e):
            for waits, fn, inc, amt in self.q[e]:
                for k, v in waits:
                    engobj.wait_ge(self._semobj(k), v)
                if fn is not None:
                    ins = fn(engobj)
                    if inc is not None:
                        ins.then_inc(self._semobj(inc), amt)

        @block.tensor
        def _(t):
            run(t, "pe")

        @block.scalar
        def _(s):
            run(s, "act")

        @block.vector
        def _(v):
            run(v, "dve")

        @block.gpsimd
        def _(g):
            run(g, "pool")

        @block.sync
        def _(s):
            run(s, "sp")


V_GMP, V_GMPOST, V_GFP, V_GFPOST, V_CW0, V_CW1, V_CW2, V_CW3, V_CB, V_BA, V_BX, V_LAM, V_PS, V_C, V_BADA = (
    0, 1, 2, 3, 4, 5, 6, 7, 8, 9, 10, 11, 12, 13, 14)
POOL_W = (2, 4, 8, 16)


def build_nc(n_main=NT_MAIN, n_pre=NT_PRE):
    nc = bass.Bass("TRN2", target_bir_lowering=False)
    dt_in = lambda name, shape: nc.dram_tensor(name, shape, F32, kind="ExternalInput").ap()
    xm = dt_in("xm", [TOK_CORE, D])
    xprev = dt_in("xprev", [NT_PRE * T, D]) if n_pre > 0 else None
    vecs = dt_in("vecs", [640, 128])
    flags = dt_in("flags", [128, 16])
    invc = dt_in("invc", [128, 64])
    ident_d = dt_in("ident", [128, 128])
    win = dt_in("win", [160, 128, 4096])
    wrg = dt_in("wrg", [16, 128, 1024])
    wpool = dt_in("wpool", [8, 128, 4096])
    wbl = dt_in("wbl", [32, 128, 4096])
    wbp = dt_in("wbp", [32, 128, 4096])
    wo = dt_in("wo", [32, 128, 4096])
    wgu = dt_in("wgu", [172, 128, 4096])
    wd = dt_in("wd", [32, 128, DFF])
    wada = dt_in("wada", [192, 128, 4096])
    y = nc.dram_tensor("y", [TOK_CORE, D], F32, kind="ExternalOutput").ap()
    ysc_d = nc.dram_tensor("ysc_d", [T, D], F32).ap()
    x1_d = nc.dram_tensor("x1_d", [T, D], F32).ap()

    with contextlib.ExitStack() as es:
        P = Prog(nc, es)
        sbt = lambda name, shape, dt: es.enter_context(nc.sbuf_tensor(name, shape, dt))
        u = sbt("u", [128, 32, T], BF16)
        bcd = sbt("bcd", [128, 96 * 256], F32)
        scr = sbt("scr", [128, 9248], F32)
        wslot = [sbt(f"wslot{i}", [128, 4096], BF16) for i in range(NW)]
        vecT = sbt("vecT", [128, 640], F32)
        modT = sbt("modT", [128, 192], F32)
        der = sbt("der", [128, 6 * 32], F32)
        nsp = sbt("nsp", [128, 64], F32)
        state = sbt("state", [128, 32], F32)
        xr_tail = sbt("xr_tail", [128, 32 * 3], F32)
        xp_tail = sbt("xp_tail", [128, 32 * 15], F32)
        identf = sbt("identf", [128, 128], F32)
        identb = sbt("identb", [128, 128], BF16)
        onesc = sbt("onesc", [128, 1], F32)
        flg = sbt("flg", [128, 16], F32)
        invt = sbt("invt", [128, 64], F32)
        cact = sbt("cact", [128, 32], BF16)
        small = sbt("small", [128, 32], F32)
        psum = [es.enter_context(nc.psum_tensor(f"ps{i}", [128, 512], F32)) for i in range(8)]
        block = es.enter_context(nc.Block())

        b_u = [Buf(f"u{c}") for c in range(32)]
        b_bcd = [Buf(f"bcd{c}") for c in range(96)]
        b_w = [Buf(f"w{i}") for i in range(NW)]
        b_ps = [Buf(f"ps{i}") for i in range(8)]
        b_vecT, b_modT, b_der, b_nsp = Buf(), Buf(), Buf(), Buf()
        b_state = [Buf() for _ in range(32)]
        b_xrt = [Buf() for _ in range(32)]
        b_xpt = [Buf() for _ in range(32)]
        b_const = Buf()
        b_small = Buf()
        b_small2 = Buf()
        b_ysc_d = [Buf() for _ in range(4)]
        b_x1_d = [Buf() for _ in range(4)]

        def bview(c):
            return bcd[:, c * 256:(c + 1) * 256].bitcast(BF16)

        xblk = [bcd[:, i * 4096:(i + 1) * 4096] for i in range(2)]
        yscp = [bcd[:, 8192 + i * 512: 8192 + (i + 1) * 512] for i in range(2)]
        xnp_ = [bcd[:, 9216 + i * 512: 9216 + (i + 1) * 512].bitcast(BF16) for i in range(2)]
        sqj = bcd[:, 10240:10752].bitcast(BF16)
        b_xblk = [Buf(), Buf()]
        b_yscp = [Buf(), Buf()]
        b_xnp = [Buf(), Buf()]
        b_sqj = Buf()

        ring = [0]
        accn = [0]
        wep = [0]

        def wload(src, ncols=4096):
            s = ring[0] % NW
            ring[0] += 1
            P.dma("pool", f"w{s}_{wep[0]}", lambda e: e.dma_start(out=wslot[s][:, 0:ncols], in_=src), writes=[b_w[s]])
            return s

        def acc():
            b = accn[0] % 4
            accn[0] += 1
            return b

        def mm_group(out_ap, out_buf, pairs, reads):
            n = len(pairs)
            for i, (l, r) in enumerate(pairs):
                P.op("pe", lambda e, l=l, r=r, i=i: e.matmul(out_ap, lhsT=l, rhs=r, start=(i == 0), stop=(i == n - 1)),
                     reads=reads, writes=[out_buf], inc=(i == n - 1))

        def act(out, in_, func, reads, writes, bias=None, scale=None, accum=None):
            kw = {}
            if bias is not None:
                kw["bias"] = bias
            if scale is not None:
                kw["scale"] = scale
            if accum is not None:
                kw["accum_out"] = accum
            return P.op("act", lambda e: e.activation(out=out, in_=in_, func=func, **kw), reads=reads, writes=writes)

        def tt(out, in0, in1, op, reads, writes, eng="dve"):
            return P.op(eng, lambda e: e.tensor_tensor(out=out, in0=in0, in1=in1, op=op), reads=reads, writes=writes)

        def ts(out, in0, s1, s2, op0, op1, reads, writes):
            if s2 is None:
                return P.op("dve", lambda e: e.tensor_scalar(out=out, in0=in0, scalar1=s1, scalar2=None, op0=op0), reads=reads, writes=writes)
            return P.op("dve", lambda e: e.tensor_scalar(out=out, in0=in0, scalar1=s1, scalar2=s2, op0=op0, op1=op1), reads=reads, writes=writes)

        def stt(out, in0, scalar, in1, op0, op1, reads, writes):
            return P.op("dve", lambda e: e.scalar_tensor_tensor(out=out, in0=in0, scalar=scalar, in1=in1, op0=op0, op1=op1), reads=reads, writes=writes)

        def cp(out, in_, reads, writes):
            return P.op("dve", lambda e: e.tensor_copy(out=out, in_=in_), reads=reads, writes=writes)

        def V(i):
            return vecT[:, i * 32:(i + 1) * 32]

        def Vc(i, c):
            return vecT[:, i * 32 + c: i * 32 + c + 1]

        P.dma("sp", "c0", lambda e: e.dma_start(out=identf[:], in_=ident_d), writes=[b_const])
        P.dma("sp", "c1", lambda e: e.dma_start(out=flg[:], in_=flags), writes=[b_const])
        P.dma("sp", "c2", lambda e: e.dma_start(out=invt[:], in_=invc), writes=[b_const])
        cp(identb[:], identf[:], [b_const], [b_const])
        P.op("dve", lambda e: e.memset(onesc[:], 1.0), writes=[b_const])
        P.op("dve", lambda e: e.memset(state[:], 0.0), writes=b_state)
        P.op("dve", lambda e: e.memset(xr_tail[:], 0.0), writes=b_xrt)
        P.op("dve", lambda e: e.memset(xp_tail[:], 0.0), writes=b_xpt)
        b_vst = [Buf() for _ in range(5)]
        for i in range(5):
            P.dma("sp", f"v{i}", lambda e, i=i: e.dma_start(out=scr[:, i * 128:(i + 1) * 128], in_=vecs[i * 128:(i + 1) * 128, :]), writes=[b_vst[i]])
        for i in range(5):
            pb = 4 + (i % 2)
            P.op("pe", lambda e, i=i, pb=pb: e.transpose(out=psum[pb][:, 0:128], in_=scr[:, i * 128:(i + 1) * 128], identity=identf[:]),
                 reads=[b_vst[i], b_const], writes=[b_ps[pb]])
            cp(vecT[:, i * 128:(i + 1) * 128], psum[pb][:, 0:128], [b_ps[pb]], [b_vecT])
        act(cact[:], V(V_C), AF.Silu, [b_vecT], [b_const])
        act(nsp[:, 0:32], V(V_LAM), AF.Exp, [b_vecT], [b_nsp], scale=-1.0)
        act(nsp[:, 0:32], nsp[:, 0:32], AF.Ln, [b_nsp], [b_nsp], bias=1.0)
        ts(nsp[:, 32:64], nsp[:, 0:32], -16.0, None, ALU.mult, None, [b_nsp], [b_nsp])
        ts(nsp[:, 0:32], nsp[:, 0:32], -8.0, None, ALU.mult, None, [b_nsp], [b_nsp])

        b_row = [Buf(), Buf()]
        rowsb = [scr[0:1, 1024 + i * 512: 1024 + (i + 1) * 512] for i in range(2)]
        rowx = sbt("rowx", [1, 512], F32)
        b_rowx = Buf()
        ada_next = [0]

        def adaln_groups(ng, in_prefix):
            for g in range(ada_next[0], min(48, ada_next[0] + ng)):
                rb = 7 if in_prefix else (7, 5)[g % 2]
                rsb, rbuf = (rowx[0:1, :], b_rowx) if in_prefix else (rowsb[g % 2], b_row[g % 2])
                for q in range(4):
                    s = wload(wada[g * 4 + q])
                    for kk in range(8):
                        kc = q * 8 + kk
                        P.op("pe", lambda e, s=s, kk=kk, kc=kc, rb=rb: e.matmul(psum[rb][0:1, :], lhsT=cact[:, kc:kc + 1], rhs=wslot[s][:, kk * 512:(kk + 1) * 512],
                                                                              start=(kc == 0), stop=(kc == 31)),
                             reads=[b_w[s], b_const], writes=[b_ps[rb]], inc=(kc == 31))
                act(rsb, psum[rb][0:1, :], AF.Copy, [b_ps[rb]], [rbuf])
                for q in range(4):
                    P.op("pe", lambda e, g=g, q=q, rsb=rsb: e.matmul(psum[6][:, g * 4 + q: g * 4 + q + 1], lhsT=rsb[0:1, q * 128:(q + 1) * 128], rhs=identf[0:1, 0:1],
                                                                    start=True, stop=True),
                         reads=[rbuf, b_const], writes=[b_ps[6]])
                ada_next[0] = g + 1

        adaln_groups(16, False)
        tt(modT[:, 0:64], psum[6][:, 0:64], vecT[:, 448:512], ALU.add, [b_ps[6], b_vecT], [b_modT])
        GM_M, SH_M, GTG_M, GM_F, SH_F, GTG_F = range(6)

        def DER(i, c):
            return der[:, i * 32 + c: i * 32 + c + 1]

        stt(der[:, 0:32], modT[:, 32:64], 1.0, V(V_GMP), ALU.add, ALU.mult, [b_modT, b_vecT], [b_der])
        cp(der[:, 32:64], modT[:, 0:32], [b_modT], [b_der])

        def adaln_finish():
            adaln_groups(48, False)
            tt(modT[:, 64:192], psum[6][:, 64:192], vecT[:, 512:640], ALU.add, [b_ps[6], b_vecT], [b_modT])
            tt(der[:, 64:96], modT[:, 64:96], V(V_GMPOST), ALU.mult, [b_modT, b_vecT], [b_der])
            stt(der[:, 96:128], modT[:, 128:160], 1.0, V(V_GFP), ALU.add, ALU.mult, [b_modT, b_vecT], [b_der])
            cp(der[:, 128:160], modT[:, 96:128], [b_modT], [b_der])
            tt(der[:, 160:192], modT[:, 160:192], V(V_GFPOST), ALU.mult, [b_modT, b_vecT], [b_der])

        def prologue_block(i, tb, gm_i, sh_i):
            xb = xblk[i]
            P.op("dve", lambda e: e.memset(small[:, 0:4], 0.0), writes=[b_small])
            for q in range(4):
                act(sqj, xb[:, q * 1024:(q + 1) * 1024], AF.Square, [b_xblk[i], b_small], [b_sqj, b_small], accum=small[:, q:q + 1])
            P.op("dve", lambda e: e.reduce_sum(out=small[:, 4:5], in_=small[:, 0:4], axis=mybir.AxisListType.X), reads=[b_small], writes=[b_small])
            act(small[:, 5:6], small[:, 4:5], AF.Sqrt, [b_small], [b_small], bias=EPS, scale=1.0 / D)
            P.op("dve", lambda e: e.reciprocal(out=small[:, 6:7], in_=small[:, 5:6]), reads=[b_small], writes=[b_small])
            for q in range(4):
                xi = q % 2
                act(xnp_[xi], xb[:, q * 1024:(q + 1) * 1024], AF.Copy, [b_xblk[i], b_small], [b_xnp[xi]], scale=small[:, 6:7])
                pb = 4 + xi
                pst = psum[pb][:, 0:512].bitcast(BF16)
                for j in range(8):
                    P.op("pe", lambda e, j=j, xi=xi, pst=pst: e.transpose(out=pst[:, j * 128:(j + 1) * 128], in_=xnp_[xi][:, j * 128:(j + 1) * 128], identity=identb[:]),
                         reads=[b_xnp[xi], b_const], writes=[b_ps[pb]])
                for j in range(8):
                    c = q * 8 + j
                    o = u[:, c, tb * 128:(tb + 1) * 128]
                    src = pst[:, j * 128:(j + 1) * 128]
                    if pb == 4:
                        ts(o, src, DER(gm_i, c), DER(sh_i, c), ALU.mult, ALU.add, [b_ps[pb], b_der], [b_u[c]])
                    else:
                        act(o, src, AF.Identity, [b_ps[pb], b_der], [b_u[c]], bias=DER(sh_i, c), scale=DER(gm_i, c))

        def prologue_from_dram(rows_ap_fn, gm_i, sh_i):
            for tb in range(4):
                i = tb % 2
                P.dma("sp", f"xb{i}", lambda e, i=i, tb=tb: e.dma_start(out=xblk[i], in_=rows_ap_fn(tb)), writes=[b_xblk[i]])
                prologue_block(i, tb, gm_i, sh_i)

        XR = lambda k: scr[:, k * 520: k * 520 + 515]
        XC = lambda k: scr[:, 2080 + k * 512: 2080 + (k + 1) * 512]
        XCB = lambda k: scr[:, 4128 + k * 256: 4128 + (k + 1) * 256].bitcast(BF16)
        TMP = lambda s_, k: scr[:, 5152 + (s_ * 4 + k) * 512: 5152 + (s_ * 4 + k + 1) * 512]
        b_xr = [Buf() for _ in range(4)]
        b_xc = [Buf() for _ in range(4)]
        b_xcb = [Buf() for _ in range(4)]
        b_tmp = [[Buf() for _ in range(4)] for _ in range(2)]

        def lru_stage_a(hd, flag_col):
            for jj in range(2):
                j = 2 * hd + jj
                k = (hd % 2) * 2 + jj
                s = wload(win[j])
                b = acc()
                mm_group(psum[b][:, :], b_ps[b], [(wslot[s][:, kc * 128:(kc + 1) * 128], u[:, kc, :]) for kc in range(32)], [b_w[s]] + b_u)
                xr = XR(k)
                act(xr[:, 3:515], psum[b][:, :], AF.Copy, [b_ps[b]], [b_xr[k]])
                cp(xr[:, 0:3], xr_tail[:, j * 3:(j + 1) * 3], [b_xrt[j]], [b_xr[k]])
                xc = XC(k)
                ts(xc, xr[:, 3:515], Vc(V_CW3, j), Vc(V_CB, j), ALU.mult, ALU.add, [b_xr[k], b_vecT], [b_xc[k]])
                for kk in (2, 1, 0):
                    stt(xc, xr[:, kk:kk + 512], Vc(V_CW0 + kk, j), xc, ALU.mult, ALU.add, [b_xr[k], b_vecT, b_xc[k]], [b_xc[k]])
                if flag_col is None:
                    cp(xr_tail[:, j * 3:(j + 1) * 3], xr[:, 512:515], [b_xr[k]], [b_xrt[j]])
                else:
                    ts(xr_tail[:, j * 3:(j + 1) * 3], xr[:, 512:515], flg[:, flag_col:flag_col + 1], None, ALU.mult, None, [b_xr[k], b_const], [b_xrt[j]])
                act(XCB(k), xc, AF.Copy, [b_xc[k]], [b_xcb[k]])

        def lru_stage_b(hd, flag_col, main):
            s = wload(wrg[hd], 1024)
            kbase = (hd % 2) * 2
            sg = None
            if main:
                sg = [wload(win[32 + 2 * hd + jo]) for jo in range(2)]
            for jo in range(2):
                j = 2 * hd + jo
                R, I, A_, M = (TMP(jo, 0), TMP(jo, 1), TMP(jo, 2), TMP(jo, 3))
                bR, bI, bA, bM = b_tmp[jo]
                for gate in range(2):
                    b = acc()
                    mm_group(psum[b][:, :], b_ps[b],
                             [(wslot[s][:, ((gate * 2 + ki) * 2 + jo) * 128:((gate * 2 + ki) * 2 + jo + 1) * 128], XCB(kbase + ki)) for ki in range(2)],
                             [b_w[s], b_xcb[kbase], b_xcb[kbase + 1]])
                    if gate == 0:
                        act(R, psum[b][:, :], AF.Sigmoid, [b_ps[b], b_vecT], [bR], bias=Vc(V_BA, j))
                    else:
                        act(I, psum[b][:, :], AF.Sigmoid, [b_ps[b], b_vecT], [bI], bias=Vc(V_BX, j))
                act(A_, R, AF.Exp, [bR, b_nsp], [bA], scale=nsp[:, j:j + 1])
                act(M, R, AF.Exp, [bR, b_nsp], [bM], scale=nsp[:, 32 + j:33 + j])
                act(M, M, AF.Sqrt, [bM], [bM], bias=1.0, scale=-1.0)
                tt(I, I, XC(kbase + jo), ALU.mult, [bI, b_xc[kbase + jo]], [bI])
                tt(I, I, M, ALU.mult, [bI, bM], [bI])
                P.op("dve", lambda e, M=M, A_=A_, I=I, j=j: e.tensor_tensor_scan(out=M, data0=A_, data1=I, initial=state[:, j:j + 1], op0=ALU.mult, op1=ALU.add),
                     reads=[bA, bI, b_state[j], bM], writes=[bM])
                if flag_col is None:
                    cp(state[:, j:j + 1], M[:, 511:512], [bM], [b_state[j]])
                else:
                    ts(state[:, j:j + 1], M[:, 511:512], flg[:, flag_col:flag_col + 1], None, ALU.mult, None, [bM, b_const], [b_state[j]])
                if main:
                    b = acc()
                    s2 = sg[jo]
                    mm_group(psum[b][:, :], b_ps[b], [(wslot[s2][:, kc * 128:(kc + 1) * 128], u[:, kc, :]) for kc in range(32)], [b_w[s2]] + b_u)
                    act(R, psum[b][:, :], AF.Gelu_apprx_tanh, [b_ps[b], bR], [bR])
                    tt(bview(j), M, R, ALU.mult, [bM, bR], [b_bcd[j]])

        def lru_branch(flag_col, main):
            lru_stage_a(0, flag_col)
            for hd in range(16):
                if hd + 1 < 16:
                    lru_stage_a(hd + 1, flag_col)
                lru_stage_b(hd, flag_col, main)

        XP = lambda k: scr[:, k * 528: k * 528 + 527]
        TA = lambda k, w_: scr[:, 1056 + (2 * k + w_) * 528: 1056 + (2 * k + w_) * 528 + 527]
        PO = lambda k: scr[:, 3168 + k * 256: 3168 + (k + 1) * 256].bitcast(BF16)
        T16 = lambda k: scr[:, 5216 + k * 16: 5216 + (k + 1) * 16]
        b_xp = [Buf(), Buf()]
        b_ta = [[Buf(), Buf()], [Buf(), Buf()]]
        b_po = [Buf() for _ in range(8)]
        b_t16 = [Buf(), Buf()]

        def pool_branch(first_tile):
            cnt = 0
            for g in range(4):
                w_ = POOL_W[g]
                for k in range(8):
                    c = 8 * g + k
                    i = cnt % 2
                    cnt += 1
                    s = wload(win[64 + c])
                    b = acc()
                    mm_group(psum[b][:, :], b_ps[b], [(wslot[s][:, kc * 128:(kc + 1) * 128], u[:, kc, :]) for kc in range(32)], [b_w[s]] + b_u)
                    xp = XP(i)
                    act(xp[:, 15:527], psum[b][:, :], AF.Copy, [b_ps[b]], [b_xp[i]])
                    cp(xp[:, 0:15], xp_tail[:, c * 15:(c + 1) * 15], [b_xpt[c]], [b_xp[i]])
                    cur, cur_b, lo, step, wi = xp, b_xp[i], 1, 1, 0
                    for lvl in range(g + 1):
                        dst, dst_b = TA(i, wi), b_ta[i][wi]
                        tt(dst[:, lo:527], cur[:, lo:527], cur[:, lo - step:527 - step], ALU.add, [cur_b], [dst_b])
                        cur, cur_b = dst, dst_b
                        step *= 2
                        lo = 2 * step - 1
                        wi ^= 1
                    stt(PO(k), cur[:, 15:527], 1.0 / w_, xp[:, 15:527], ALU.mult, ALU.subtract, [cur_b, b_xp[i]], [b_po[k]])
                    if first_tile:
                        tt(T16(i), cur[:, 15:31], invt[:, g * 16:(g + 1) * 16], ALU.mult, [cur_b, b_const], [b_t16[i]])
                        tt(PO(k)[:, 0:16], T16(i), xp[:, 15:31], ALU.subtract, [b_t16[i], b_xp[i]], [b_po[k]])
                    act(xp_tail[:, c * 15:(c + 1) * 15], xp[:, 512:527], AF.Copy, [b_xp[i]], [b_xpt[c]])
                for nh in range(2):
                    s = wload(wpool[g * 2 + nh])
                    for n4 in range(4):
                        n = 8 * g + nh * 4 + n4
                        b = acc()
                        mm_group(psum[b][:, :], b_ps[b],
                                 [(wslot[s][:, kc * 512 + n4 * 128: kc * 512 + (n4 + 1) * 128], PO(kc)) for kc in range(8)],
                                 [b_w[s]] + b_po)
                        act(bview(32 + n), psum[b][:, :], AF.Copy, [b_ps[b], b_vecT], [b_bcd[32 + n]], scale=Vc(V_PS, n))

        S1 = lambda i, k: scr[:, (k * 2 + i) * 512: (k * 2 + i + 1) * 512]
        b_m3 = [[Buf() for _ in range(4)] for _ in range(2)]

        def merge_phase():
            rhs_u = [u[:, kc, :] for kc in range(32)]
            for n in range(32):
                i = n % 2
                s1, t1, s3, t2 = (S1(i, k) for k in range(4))
                bs1, bt1, bs3, bt2 = b_m3[i]
                for br in range(2):
                    sw = wload(win[96 + 32 * br + n])
                    b = acc()
                    mm_group(psum[b][:, :], b_ps[b], [(wslot[sw][:, kc * 128:(kc + 1) * 128], rhs_u[kc]) for kc in range(32)], [b_w[sw]] + b_u)
                    act(s1 if br == 0 else s3, psum[b][:, :], AF.Sigmoid, [b_ps[b]], [bs1 if br == 0 else bs3])
                    sw = wload((wbl if br == 0 else wbp)[n])
                    b = acc()
                    yb = b_bcd[32 * br:32 * br + 32]
                    mm_group(psum[b][:, :], b_ps[b], [(wslot[sw][:, kc * 128:(kc + 1) * 128], bview(32 * br + kc)) for kc in range(32)], [b_w[sw]] + yb)
                    if br == 0:
                        tt(t1, s1, psum[b][:, :], ALU.mult, [bs1, b_ps[b]], [bt1])
                    else:
                        tt(t2, s3, psum[b][:, :], ALU.mult, [bs3, b_ps[b]], [bt2])
                tt(bview(64 + n), t1, t2, ALU.add, [bt1, bt2], [b_bcd[64 + n]])

        YSC = lambda i: scr[:, 4096 + i * 512: 4096 + (i + 1) * 512]
        SQ = lambda i: scr[:, 5120 + i * 512: 5120 + (i + 1) * 512]
        YTK = lambda i: scr[:, 6144 + i * 512: 6144 + (i + 1) * 512]
        b_ysc = [Buf(), Buf()]
        b_sq = [Buf(), Buf()]
        b_ytk = [Buf(), Buf()]
        ysc_d_v = ysc_d.rearrange("(tb p) f -> p tb f", p=128)
        ysc_toks = [None, None]

        def epi_chunk(n, b, gtg_i):
            i = n % 2
            act(YSC(i), psum[b][:, :], AF.Copy, [b_ps[b], b_der], [b_ysc[i]], scale=DER(gtg_i, n))
            act(SQ(i), psum[b][:, :], AF.Square, [b_ps[b]], [b_sq[i]])
            P.op("pe", lambda e, i=i, n=n: e.matmul(psum[6][0:1, :], lhsT=onesc[:, 0:1], rhs=SQ(i), start=(n == 0), stop=(n == 31)),
                 reads=[b_sq[i], b_const], writes=[b_ps[6]])
            pb = 4 + i
            for tb in range(4):
                P.op("pe", lambda e, i=i, tb=tb, pb=pb: e.transpose(out=psum[pb][:, tb * 128:(tb + 1) * 128], in_=YSC(i)[:, tb * 128:(tb + 1) * 128], identity=identf[:]),
                     reads=[b_ysc[i], b_const], writes=[b_ps[pb]])
            cp(YTK(i), psum[pb][:, :], [b_ps[pb]], [b_ytk[i]])
            ysc_toks[i] = P.dma("sp", f"yk{i}", lambda e, i=i, n=n: e.dma_start(out=ysc_d_v[:, :, n * 128:(n + 1) * 128], in_=YTK(i).rearrange("p (tb f) -> p tb f", tb=4)),
                  reads=[b_ytk[i]])

        def epi_rstd():
            row = scr[0:1, 7168:7680]
            b_r = Buf()
            act(row, psum[6][0:1, :], AF.Sqrt, [b_ps[6]], [b_r], bias=EPS, scale=1.0 / D)
            P.op("dve", lambda e: e.reciprocal(out=row, in_=row), reads=[b_r], writes=[b_r])
            for tb in range(4):
                P.op("pe", lambda e, tb=tb: e.matmul(psum[7][:, tb:tb + 1], lhsT=row[0:1, tb * 128:(tb + 1) * 128], rhs=identf[0:1, 0:1], start=True, stop=True),
                     reads=[b_r, b_const], writes=[b_ps[7]])
            cp(small[:, 12:16], psum[7][:, 0:4], [b_ps[7]], [b_small2])

        def pass2_block(tb, i, xsrc_ap, xsrc_bufs):
            P.dma("sp", f"xb{i}", lambda e: e.dma_start(out=xblk[i], in_=xsrc_ap), reads=xsrc_bufs, writes=[b_xblk[i]])
            for q in range(8):
                yi = q % 2
                P.dma("sp", f"yp{yi}", lambda e, yi=yi, q=q: e.dma_start(out=yscp[yi], in_=ysc_d[tb * 128:(tb + 1) * 128, q * 512:(q + 1) * 512]),
                      writes=[b_yscp[yi]], extra=list(ysc_toks))
                stt(xblk[i][:, q * 512:(q + 1) * 512], yscp[yi], small[:, 12 + tb:13 + tb], xblk[i][:, q * 512:(q + 1) * 512], ALU.mult, ALU.add,
                    [b_yscp[yi], b_small2, b_xblk[i]], [b_xblk[i]])

        for pt in range(NT_PRE - n_pre, NT_PRE):
            if pt == NT_PRE - n_pre:
                P.fence()
            prologue_from_dram(lambda tb, pt=pt: xprev[pt * T + tb * 128: pt * T + (tb + 1) * 128, :], GM_M, SH_M)
            lru_branch(pt, False)
            if pt < NT_PRE - 1:
                adaln_groups(4, True)
            if pt == NT_PRE - 1:
                for c in range(32):
                    s = wload(win[64 + c])
                    hb = (7, 5)[c % 2]
                    hr = psum[hb][:, 0:16]
                    mm_group(hr, b_ps[hb], [(wslot[s][:, kc * 128:(kc + 1) * 128], u[:, kc, 496:512]) for kc in range(32)], [b_w[s]] + b_u)
                    ts(xp_tail[:, c * 15:(c + 1) * 15], hr[:, 1:16], flg[:, pt:pt + 1], None, ALU.mult, None, [b_ps[hb], b_const], [b_xpt[c]])

        P.fence()
        adaln_finish()
        out_toks = []
        for t in range(n_main):
            P.fence()
            if t % 4 == 0:
                P.new_epoch()
                wep[0] = 1 + t // 4
            prologue_from_dram(lambda tb, t=t: xm[t * T + tb * 128: t * T + (tb + 1) * 128, :], GM_M, SH_M)
            P.fence()
            lru_branch(None, True)
            P.fence()
            pool_branch(t == 0)
            P.fence()
            merge_phase()
            for n in range(32):
                s = wload(wo[n])
                b = acc()
                mm_group(psum[b][:, :], b_ps[b], [(wslot[s][:, kc * 128:(kc + 1) * 128], bview(64 + kc)) for kc in range(32)], [b_w[s]] + b_bcd[64:96])
                epi_chunk(n, b, GTG_M)
            epi_rstd()
            P.fence()
            for tb in range(4):
                i = tb % 2
                pass2_block(tb, i, xm[t * T + tb * 128: t * T + (tb + 1) * 128, :], [])
                P.dma("sp", f"xs{i}", lambda e, i=i, tb=tb: e.dma_start(out=x1_d[tb * 128:(tb + 1) * 128, :], in_=xblk[i]), reads=[b_xblk[i]], writes=[b_x1_d[tb]])
                prologue_block(i, tb, GM_F, SH_F)
            P.fence()
            b_sg = [Buf(), Buf()]
            rhs_u = [u[:, kc, :] for kc in range(32)]
            for j in range(HC):
                i = j % 2
                sG = wload(wgu[j])
                bG = acc()
                mm_group(psum[bG][:, :], b_ps[bG], [(wslot[sG][:, kc * 128:(kc + 1) * 128], rhs_u[kc]) for kc in range(32)], [b_w[sG]] + b_u)
                sgv = scr[:, i * 512:(i + 1) * 512]
                act(sgv, psum[bG][:, :], AF.Silu, [b_ps[bG]], [b_sg[i]])
                sU = wload(wgu[HC + j])
                bU = acc()
                mm_group(psum[bU][:, :], b_ps[bU], [(wslot[sU][:, kc * 128:(kc + 1) * 128], rhs_u[kc]) for kc in range(32)], [b_w[sU]] + b_u)
                tt(bview(j), sgv, psum[bU][:, :], ALU.mult, [b_sg[i], b_ps[bU]], [b_bcd[j]])
            for n in range(32):
                ss = [wload(wd[n][:, 0:4096]), wload(wd[n][:, 4096:8192]), wload(wd[n][:, 8192:DFF], DFF - 8192)]
                b = acc()
                mm_group(psum[b][:, :], b_ps[b],
                         [(wslot[ss[kc // 32]][:, (kc % 32) * 128:(kc % 32 + 1) * 128], bview(kc)) for kc in range(HC)],
                         [b_w[s_] for s_ in ss] + b_bcd[0:HC])
                epi_chunk(n, b, GTG_F)
            epi_rstd()
            P.fence()
            for tb in range(4):
                i = tb % 2
                pass2_block(tb, i, x1_d[tb * 128:(tb + 1) * 128, :], [b_x1_d[tb]])
                out_toks.append(P.dma("sp", f"xs{i}", lambda e, i=i, tb=tb, t=t: e.dma_start(out=y[t * T + tb * 128: t * T + (tb + 1) * 128, :], in_=xblk[i]),
                                      reads=[b_xblk[i]]))
        P.fence()
        P.wait_all("sp", out_toks)
        P.emit(block)
    return nc


def _tile_cols(w):
    K, N = w.shape
    return np.ascontiguousarray(w.reshape(K // 128, 128, N // 128, 128).transpose(2, 1, 0, 3)).reshape(N // 128, 128, K)


def _prep_shared(inp):
    sh = {}
    sh["win"] = _tile_cols(inp["w_in"][0])
    wa = inp["w_rg_a"][0].reshape(16, 2, 128, 2, 128).transpose(0, 2, 1, 3, 4)
    wx = inp["w_rg_x"][0].reshape(16, 2, 128, 2, 128).transpose(0, 2, 1, 3, 4)
    sh["wrg"] = np.ascontiguousarray(np.stack([wa, wx], axis=2)).reshape(16, 128, 1024)
    sh["wpool"] = np.ascontiguousarray(inp["pool_w"][0].reshape(4, 8, 128, 2, 512).transpose(0, 3, 2, 1, 4)).reshape(8, 128, 4096)
    sh["wbl"] = _tile_cols(inp["w_branch_lru"][0])
    sh["wbp"] = _tile_cols(inp["w_branch_pool"][0])
    sh["wo"] = _tile_cols(inp["w_o"][0])
    sh["wgu"] = _tile_cols(inp["w_gate_up"][0])
    sh["wd"] = _tile_cols(inp["w_down"][0])
    sh["wada"] = np.ascontiguousarray(inp["w_ada"][0].reshape(4, 8, 128, 48, 512).transpose(3, 0, 2, 1, 4)).reshape(192, 128, 4096)
    sh["ident"] = np.eye(128, dtype=np.float32)
    return sh


def _vec_rows(inp, b):
    rows = [inp["g_mix_pre"][0], inp["g_mix_post"][0], inp["g_ffn_pre"][0], inp["g_ffn_post"][0],
            inp["conv_w"][0, 0], inp["conv_w"][0, 1], inp["conv_w"][0, 2], inp["conv_w"][0, 3],
            inp["conv_b"][0], inp["b_rg_a"][0], inp["b_rg_x"][0], inp["lru_lambda"][0], inp["pool_scale"][0],
            inp["c"][b]]
    v = np.concatenate([np.asarray(r, np.float32).reshape(32, 128) for r in rows] + [np.asarray(inp["b_ada"][0], np.float32).reshape(192, 128)], axis=0)
    return np.ascontiguousarray(v)


def _core_inputs(inp, sh, c):
    b, j = divmod(c, 4)
    x = inp["x"]
    m = dict(sh)
    m["xm"] = np.ascontiguousarray(x[b, j * TOK_CORE:(j + 1) * TOK_CORE, :])
    xp = np.zeros((NT_PRE * T, D), np.float32)
    start = j * TOK_CORE - NT_PRE * T
    if j > 0:
        xp[-j * TOK_CORE:, :] = x[b, 0:j * TOK_CORE, :]
    m["xprev"] = xp
    fl = np.zeros((128, 16), np.float32)
    for pt in range(NT_PRE):
        if start + pt * T >= 0:
            fl[:, pt] = 1.0
    m["flags"] = fl
    iv = np.zeros((128, 64), np.float32)
    for g, w_ in enumerate(POOL_W):
        for t in range(16):
            iv[:, g * 16 + t] = 1.0 / (min(t + 1, w_) if j == 0 else w_)
    m["invc"] = iv
    m["vecs"] = _vec_rows(inp, b)
    return m


_NC_CACHE = {}


def kernel(**inputs):
    inp = {k: np.asarray(v, dtype=np.float32) for k, v in inputs.items()}
    sh = _prep_shared(inp)
    in_maps = [_core_inputs(inp, sh, c) for c in range(N_CORES)]
    if "nc" not in _NC_CACHE:
        _NC_CACHE["nc"] = build_nc()
    nc = _NC_CACHE["nc"]
    res = run_bass_kernel_spmd(nc, in_maps, core_ids=list(range(N_CORES)))
    out = np.empty((2, 8192, D), np.float32)
    for c in range(N_CORES):
        b, j = divmod(c, 4)
        out[b, j * TOK_CORE:(j + 1) * TOK_CORE, :] = res.results[c]["y"]
    return out
```
